# Optimizing a Trainium2 kernel written in Bass

```python
import math
import jax, jax.numpy as jnp
from jax import lax
import numpy as np

D_MODEL = 1024
BATCH = 16
SEQ = 2048
DEPTH = 2

HEAD_DIM = 64
A_HEADS = 6
A_PAIRS = ((128, 1), (512, 4), (2048, 16))
B_HEADS = 4
B_QK_DIM = 32
B_V_DIM = 2 * B_QK_DIM
C_HEADS = 6
C_Q_LORA = 256
C_KV_LORA = 128
C_NOPE = 64
C_ROPE = 32
C_V = 64
ROPE_THETA = 10000.0
MIX_WIDTH = (A_HEADS + B_HEADS + C_HEADS) * HEAD_DIM
A_COLS = 3 * A_HEADS * HEAD_DIM
B_COLS = B_HEADS * (2 * 2 * B_QK_DIM + B_V_DIM)
C_COLS = C_Q_LORA + C_KV_LORA + C_ROPE
IN_COLS = A_COLS + B_COLS + C_COLS
NUM_BUCKETS = 32
MAX_DISTANCE = 2048
BIAS_HEADS = A_HEADS + B_HEADS
FF_DIM = -(-8 * D_MODEL // (3 * 256)) * 256
Q_BLOCK = 128
DEEPNORM_ALPHA = (2 * DEPTH) ** 0.25
DEEPNORM_BETA = (8 * DEPTH) ** -0.25
LN_EPS = 1e-5
LATENT_EPS = 1e-6
SUBLN_EPS = 1e-5

kernel_name = "hybrid_dilated_diff_mla_deepnorm"


def layer_norm(x, g, b):
    xf = x.astype(jnp.float32)
    mu = jnp.mean(xf, axis=-1, keepdims=True)
    var = jnp.mean(jnp.square(xf - mu), axis=-1, keepdims=True)
    return ((xf - mu) * lax.rsqrt(var + LN_EPS) * g + b).astype(x.dtype)


def rms_norm(x, g, eps):
    xf = x.astype(jnp.float32)
    return (xf * lax.rsqrt(jnp.mean(xf * xf, axis=-1, keepdims=True) + eps) * g).astype(x.dtype)


def t5_bucket(dist):
    max_exact = NUM_BUCKETS // 2
    d = jnp.maximum(dist, 0)
    df = jnp.maximum(d, 1).astype(jnp.float32)
    large = max_exact + (jnp.log(df / max_exact) / math.log(MAX_DISTANCE / max_exact)
                         * (NUM_BUCKETS - max_exact)).astype(jnp.int32)
    large = jnp.minimum(large, NUM_BUCKETS - 1)
    return jnp.where(d < max_exact, d, large)


def apply_rope(x):
    S, half = x.shape[1], x.shape[-1] // 2
    inv = ROPE_THETA ** (-jnp.arange(half, dtype=jnp.float32) / half)
    ang = jnp.arange(S, dtype=jnp.float32)[:, None] * inv[None, :]
    cos, sin = jnp.cos(ang)[None, :, None, :], jnp.sin(ang)[None, :, None, :]
    xf = x.astype(jnp.float32)
    x1, x2 = xf[..., :half], xf[..., half:]
    return jnp.concatenate([x1 * cos - x2 * sin, x2 * cos + x1 * sin], axis=-1).astype(x.dtype)


def dilated_pair(q, k, v, bias_tab, window, dil):
    Bn, S, H, Dh = q.shape
    blk = window // dil
    L = S // dil
    nb = -(-L // blk)
    pad = nb * blk - L

    def to_blocks(a):
        a = a.reshape(Bn, L, dil, H, Dh)
        a = jnp.pad(a, ((0, 0), (0, pad), (0, 0), (0, 0), (0, 0)))
        return a.reshape(Bn, nb, blk, dil, H, Dh)

    def with_prev(a):
        prev = jnp.pad(a, ((0, 0), (1, 0), (0, 0), (0, 0), (0, 0), (0, 0)))[:, :nb]
        return jnp.concatenate([prev, a], axis=2)

    qb = to_blocks(q)
    kb = with_prev(to_blocks(k))
    vb = with_prev(to_blocks(v))
    rel = jnp.arange(blk)[:, None] + blk - jnp.arange(2 * blk)[None, :]
    band = (rel >= 0) & (rel <= blk)
    not_first = (jnp.arange(nb)[:, None, None] > 0) | (jnp.arange(2 * blk)[None, None, :] >= blk)
    mask = band[None] & not_first
    bias = bias_tab[t5_bucket(rel * dil)].astype(jnp.float32).transpose(2, 0, 1)
    s = jnp.einsum('bnqrhd,bnkrhd->bnrhqk', qb, kb).astype(jnp.float32) * (Dh ** -0.5) + bias
    s = jnp.where(mask[None, :, None, None], s, -jnp.inf)
    lse = jax.nn.logsumexp(s, axis=-1)
    p = jnp.exp(s - lse[..., None]).astype(v.dtype)
    o = jnp.einsum('bnrhqk,bnkrhd->bnqrhd', p, vb)
    o = o.reshape(Bn, nb * blk, dil, H, Dh)[:, :L].reshape(Bn, S, H, Dh)
    lse = lse.transpose(0, 1, 4, 2, 3).reshape(Bn, nb * blk, dil, H)[:, :L].reshape(Bn, S, H)
    return o, lse


def dilated_mixture(q, k, v, bias_tab):
    outs, lses = [], []
    for window, dil in A_PAIRS:
        o, lse = dilated_pair(q, k, v, bias_tab, window, dil)
        outs.append(o)
        lses.append(lse)
    w = jax.nn.softmax(jnp.stack(lses, axis=0), axis=0)
    o = jnp.sum(w[..., None] * jnp.stack(outs, axis=0).astype(jnp.float32), axis=0)
    return o.astype(q.dtype)


def sweep_query_blocks(block_fn, n_pos):
    o = lax.map(block_fn, jnp.arange(n_pos // Q_BLOCK))
    o = jnp.moveaxis(o, 0, 1)
    return o.reshape((o.shape[0], n_pos) + o.shape[3:])


def diff_attention(q1, q2, k1, k2, v, lam, bias_tab):
    S = q1.shape[1]
    scale = q1.shape[-1] ** -0.5
    kpos = jnp.arange(S)

    def block(i):
        start = i * Q_BLOCK
        sl = lambda a: lax.dynamic_slice_in_dim(a, start, Q_BLOCK, axis=1)
        rel = (start + jnp.arange(Q_BLOCK))[:, None] - kpos[None, :]
        causal = rel >= 0
        bias = bias_tab[t5_bucket(rel)].astype(jnp.float32).transpose(2, 0, 1)

        def probs(q, k):
            s = jnp.einsum('bqhd,bkhd->bhqk', sl(q), k).astype(jnp.float32) * scale + bias
            return jax.nn.softmax(jnp.where(causal, s, -jnp.inf), axis=-1)

        a = probs(q1, k1) - lam * probs(q2, k2)
        return jnp.einsum('bhqk,bkhd->bqhd', a.astype(v.dtype), v)

    return sweep_query_blocks(block, S)


def causal_attention(q, k, v):
    S = q.shape[1]
    scale = q.shape[-1] ** -0.5
    kpos = jnp.arange(S)

    def block(i):
        start = i * Q_BLOCK
        qs = lax.dynamic_slice_in_dim(q, start, Q_BLOCK, axis=1)
        causal = (start + jnp.arange(Q_BLOCK))[:, None] >= kpos[None, :]
        s = jnp.einsum('bqhd,bkhd->bhqk', qs, k).astype(jnp.float32) * scale
        p = jax.nn.softmax(jnp.where(causal, s, -jnp.inf), axis=-1)
        return jnp.einsum('bhqk,bkhd->bqhd', p.astype(v.dtype), v)

    return sweep_query_blocks(block, S)


def hybrid_layer(x, layer_idx, rel_bias, w_in, q_norm_g, kv_norm_g, w_uq, w_ukv,
                 diff_lambda, subln_g, w_o, ln1_g, ln1_b, ln2_g, ln2_b, w_gate, w_up, w_down):
    Bn, S, _ = x.shape
    proj = x @ w_in
    a_in, b_in, c_in = jnp.split(proj, [A_COLS, A_COLS + B_COLS], axis=-1)

    a = a_in.reshape(Bn, S, 3, A_HEADS, HEAD_DIM)
    o_a = dilated_mixture(a[:, :, 0], a[:, :, 1], a[:, :, 2], rel_bias[:, :A_HEADS])

    qk_cols = B_HEADS * 2 * B_QK_DIM
    bq = b_in[..., :qk_cols].reshape(Bn, S, B_HEADS, 2, B_QK_DIM)
    bk = b_in[..., qk_cols:2 * qk_cols].reshape(Bn, S, B_HEADS, 2, B_QK_DIM)
    bv = b_in[..., 2 * qk_cols:].reshape(Bn, S, B_HEADS, B_V_DIM)
    lam_init = 0.8 - 0.6 * math.exp(-0.3 * layer_idx)
    lf = diff_lambda.astype(jnp.float32)
    lam = jnp.exp(jnp.sum(lf[0] * lf[1])) - jnp.exp(jnp.sum(lf[2] * lf[3])) + lam_init
    o_b = diff_attention(bq[..., 0, :], bq[..., 1, :], bk[..., 0, :], bk[..., 1, :], bv, lam,
                         rel_bias[:, A_HEADS:])
    o_b = (rms_norm(o_b, subln_g, SUBLN_EPS) * (1.0 - lam_init)).astype(x.dtype)

    c_q, c_kv, k_r = jnp.split(c_in, [C_Q_LORA, C_Q_LORA + C_KV_LORA], axis=-1)
    q = (rms_norm(c_q, q_norm_g, LATENT_EPS) @ w_uq).reshape(Bn, S, C_HEADS, C_NOPE + C_ROPE)
    kv = (rms_norm(c_kv, kv_norm_g, LATENT_EPS) @ w_ukv).reshape(Bn, S, C_HEADS, C_NOPE + C_V)
    q_c = jnp.concatenate([q[..., :C_NOPE], apply_rope(q[..., C_NOPE:])], axis=-1)
    k_rope = apply_rope(k_r[:, :, None, :])
    k_c = jnp.concatenate([kv[..., :C_NOPE],
                           jnp.broadcast_to(k_rope, (Bn, S, C_HEADS, C_ROPE))], axis=-1)
    o_c = causal_attention(q_c, k_c, kv[..., C_NOPE:])

    heads = jnp.concatenate([o_a.reshape(Bn, S, -1), o_b.reshape(Bn, S, -1),
                             o_c.reshape(Bn, S, -1)], axis=-1)
    h = layer_norm(DEEPNORM_ALPHA * x + heads @ w_o, ln1_g, ln1_b)

    ffn = (jax.nn.silu(h @ w_gate) * (h @ w_up)) @ w_down
    return layer_norm(DEEPNORM_ALPHA * h + ffn, ln2_g, ln2_b)


def setup_inputs(seed: int = 0) -> dict:
    key = jax.random.key(seed)
    ks = jax.random.split(key, 20)
    f32 = jnp.float32
    nrm = lambda k, shape, scale: jax.random.normal(k, shape, f32) * scale
    return {
        "x": nrm(ks[0], (BATCH, SEQ, D_MODEL), 1.0),
        "rel_bias": nrm(ks[1], (NUM_BUCKETS, BIAS_HEADS), 0.5),
        "w_in": nrm(ks[2], (DEPTH, D_MODEL, IN_COLS), D_MODEL ** -0.5),
        "q_norm_g": 1.0 + nrm(ks[3], (DEPTH, C_Q_LORA), 0.05),
        "kv_norm_g": 1.0 + nrm(ks[4], (DEPTH, C_KV_LORA), 0.05),
        "w_uq": nrm(ks[5], (DEPTH, C_Q_LORA, C_HEADS * (C_NOPE + C_ROPE)), C_Q_LORA ** -0.5),
        "w_ukv": nrm(ks[6], (DEPTH, C_KV_LORA, C_HEADS * (C_NOPE + C_V)), C_KV_LORA ** -0.5),
        "diff_lambda": nrm(ks[7], (DEPTH, 4, B_QK_DIM), 0.1),
        "subln_g": 1.0 + nrm(ks[8], (DEPTH, B_V_DIM), 0.05),
        "w_o": nrm(ks[9], (DEPTH, MIX_WIDTH, D_MODEL), MIX_WIDTH ** -0.5 * DEEPNORM_BETA),
        "ln1_g": 1.0 + nrm(ks[10], (DEPTH, D_MODEL), 0.05),
        "ln1_b": nrm(ks[11], (DEPTH, D_MODEL), 0.02),
        "ln2_g": 1.0 + nrm(ks[12], (DEPTH, D_MODEL), 0.05),
        "ln2_b": nrm(ks[13], (DEPTH, D_MODEL), 0.02),
        "w_gate": nrm(ks[14], (DEPTH, D_MODEL, FF_DIM), D_MODEL ** -0.5),
        "w_up": nrm(ks[15], (DEPTH, D_MODEL, FF_DIM), D_MODEL ** -0.5),
        "w_down": nrm(ks[16], (DEPTH, FF_DIM, D_MODEL), FF_DIM ** -0.5 * DEEPNORM_BETA),
    }


def reference(x, rel_bias, w_in, q_norm_g, kv_norm_g, w_uq, w_ukv, diff_lambda, subln_g, w_o,
              ln1_g, ln1_b, ln2_g, ln2_b, w_gate, w_up, w_down):
    for l in range(DEPTH):
        x = hybrid_layer(x, l, rel_bias, w_in[l], q_norm_g[l], kv_norm_g[l], w_uq[l], w_ukv[l],
                         diff_lambda[l], subln_g[l], w_o[l], ln1_g[l], ln1_b[l], ln2_g[l], ln2_b[l],
                         w_gate[l], w_up[l], w_down[l])
    return x
```

```python
import math
from contextlib import ExitStack
import numpy as np
import concourse.bass as bass
import concourse.mybir as mybir
from concourse.bass_utils import run_bass_kernel_spmd

F32 = mybir.dt.float32
BF16 = mybir.dt.bfloat16
ALU = mybir.AluOpType
AF = mybir.ActivationFunctionType

S = 2048
D = 1024
NSEQ = 2
DEPTH = 2
INC = 2336
FF = 2816
NFC = 22
ALPHA = (2 * DEPTH) ** 0.25
LN_EPS = 1e-5
A_PAIRS = ((128, 1), (512, 4), (2048, 16))
DEBUG_STAGE = None
BUCKET_RINT = False


def t5_bucket_np(d, force_trunc=False):
    d = np.maximum(np.asarray(d, dtype=np.int64), 0)
    df = np.maximum(d, 1).astype(np.float32)
    val = np.log(df / np.float32(16)) / np.float32(math.log(128.0)) * np.float32(16)
    large = 16 + (np.rint(val).astype(np.int32) if (BUCKET_RINT and not force_trunc) else val.astype(np.int32))
    large = np.minimum(large, 31)
    return np.where(d < 16, d, large)


def bucket_ranges(dmax, dil, force_trunc=False):
    rel = np.arange(0, dmax + 1)
    b = t5_bucket_np(rel * dil, force_trunc)
    out = []
    st = 0
    for i in range(1, len(rel) + 1):
        if i == len(rel) or b[i] != b[st]:
            out.append((int(b[st]), st, i))
            st = i
    return out


class View:
    __slots__ = ("name", "ap", "p0", "p1", "f0", "f1")

    def __init__(self, name, ap, p0, p1, f0, f1):
        self.name, self.ap, self.p0, self.p1, self.f0, self.f1 = name, ap, p0, p1, f0, f1


class Buf:
    def __init__(self, name, t, P, F, track=True):
        self.name, self.t, self.P, self.F, self.track = name, t, P, F, track
        self.th = t[:].tensor if hasattr(t, "__getitem__") else t

    def v(self, p0, p1, f0, f1, step=1):
        if step == 1:
            ap = bass.AP(self.th, p0 * self.F + f0, [[self.F, p1 - p0], [1, f1 - f0]])
            return View(self.name, ap, p0, p1, f0, f1)
        n = (f1 - f0 + step - 1) // step
        ap = bass.AP(self.th, p0 * self.F + f0, [[self.F, p1 - p0], [step, n]])
        return View(self.name, ap, p0, p1, f0, f0 + (n - 1) * step + 1)

    def v3(self, p0, p1, f0, n1, s1, n2, s2=1):
        ap = bass.AP(self.th, p0 * self.F + f0, [[self.F, p1 - p0], [s1, n1], [s2, n2]])
        return View(self.name, ap, p0, p1, f0, f0 + (n1 - 1) * s1 + (n2 - 1) * s2 + 1)


class DBuf:
    def __init__(self, name, th, track=False):
        self.name, self.th, self.track = name, th, track

    def v(self, off, pattern):
        ap = bass.AP(self.th, off, [list(p) for p in pattern])
        hi = off + sum((n - 1) * abs(s) for s, n in pattern) + 1
        return View(self.name if self.track else None, ap, 0, 1, off, hi)


ENGS = ("pe", "act", "dve", "pool", "sp")
NDMASEM = 8


class Prog:
    def __init__(self):
        self.ops = []
        self.phase = 0
        self.pslast = {}
        self.marks = []
        self.hist = {}

    def new_phase(self):
        self.phase += 1

    def mark(self, label):
        self.marks.append((label, sum(1 for o in self.ops if o['eng'] == 'pe')))

    def op(self, eng, fn, reads=(), writes=(), dma=False):
        idx = len(self.ops)
        deps = set()
        pacc = {}
        for r in reads:
            if r.name is not None and r.name.startswith("ps"):
                pacc.setdefault(r.name, False)
        for w in writes:
            if w.name is not None and w.name.startswith("ps"):
                pacc[w.name] = True
        for bank, is_w in pacc.items():
            last = self.pslast.setdefault(bank, {})
            for e2, (i2, w2) in last.items():
                if e2 != eng or is_w or w2:
                    deps.add(i2)
            last[eng] = (idx, is_w)
        reads = [r for r in reads if not (r.name is not None and r.name.startswith("ps"))]
        writes = [w for w in writes if not (w.name is not None and w.name.startswith("ps"))]
        for r in reads:
            if r.name is None:
                continue
            for e in self.hist.get(r.name, ()):
                if e[5] and e[0] < r.p1 and r.p0 < e[1] and e[2] < r.f1 and r.f0 < e[3]:
                    deps.add(e[4])
        for w in writes:
            if w.name is None:
                continue
            for e in self.hist.get(w.name, ()):
                if e[0] < w.p1 and w.p0 < e[1] and e[2] < w.f1 and w.f0 < e[3]:
                    deps.add(e[4])
        for w in writes:
            if w.name is None:
                continue
            h = self.hist.setdefault(w.name, [])
            h[:] = [e for e in h if not (w.p0 <= e[0] and e[1] <= w.p1 and w.f0 <= e[2] and e[3] <= w.f1)]
            h.append([w.p0, w.p1, w.f0, w.f1, idx, True, eng, dma])
        for r in reads:
            if r.name is None:
                continue
            h = self.hist.setdefault(r.name, [])
            merged = False
            if not dma:
                for e in h:
                    if (not e[5]) and e[6] == eng and not e[7] and e[0] == r.p0 and e[1] == r.p1 and e[2] == r.f0 and e[3] == r.f1:
                        e[4] = idx
                        merged = True
                        break
            if not merged:
                h.append([r.p0, r.p1, r.f0, r.f1, idx, False, eng, dma])
                if len(h) > 96:
                    self._compact(h)
        self.ops.append(dict(eng=eng, fn=fn, deps=deps, dma=dma, phase=self.phase, sig=False))
        return idx

    def _compact(self, h):
        keep = [e for e in h if e[5] or e[7]]
        per = {}
        for e in h:
            if e[5] or e[7]:
                continue
            m = per.get(e[6])
            if m is None:
                per[e[6]] = list(e)
            else:
                m[0] = min(m[0], e[0]); m[1] = max(m[1], e[1]); m[2] = min(m[2], e[2]); m[3] = max(m[3], e[3])
                m[4] = max(m[4], e[4])
        h[:] = keep + list(per.values())

    def emit(self, nc, es):
        ops = self.ops
        nph = self.phase + 1
        for o in ops:
            for d in o["deps"]:
                p = ops[d]
                if p["dma"]:
                    continue
                if p["eng"] == "pe" and o["eng"] == "pe" and not o["dma"]:
                    continue
                p["sig"] = True
        sems = {}
        for e in ENGS:
            for ph in range(nph):
                sems[(e, ph)] = es.enter_context(nc.semaphore(f"s_{e}_{ph}"))
        dsems = {}
        for e in ("sp", "pool", "act"):
            for k in range(NDMASEM):
                dsems[(e, k)] = es.enter_context(nc.semaphore(f"d_{e}_{k}"))
        cnt = {}
        dcnt = {}
        dn = {e: 0 for e in ENGS}
        for o in ops:
            if o["dma"]:
                k = dn[o["eng"]] % NDMASEM
                dn[o["eng"]] += 1
                key = (o["eng"], k)
                o["dprev"] = dcnt.get(key, 0)
                dcnt[key] = o["dprev"] + 16
                o["dsem"] = key
                o["dval"] = dcnt[key]
            elif o["sig"]:
                key = (o["eng"], o["phase"])
                cnt[key] = cnt.get(key, 0) + 1
                o["sval"] = cnt[key]
        per_eng = {e: [] for e in ENGS}
        for i, o in enumerate(ops):
            per_eng[o["eng"]].append(i)
        block = es.enter_context(nc.Block())

        def run(engname, e):
            waited = {}
            for i in per_eng[engname]:
                o = ops[i]
                need = {}
                for d in o["deps"]:
                    p = ops[d]
                    if p["dma"]:
                        key = ("d",) + p["dsem"]
                        val = p["dval"]
                    else:
                        if p["eng"] == "pe" and engname == "pe" and not o["dma"]:
                            continue
                        key = ("s", p["eng"], p["phase"])
                        val = p["sval"]
                    if need.get(key, 0) < val:
                        need[key] = val
                if o["dma"] and o["dprev"] > 0:
                    key = ("d",) + o["dsem"]
                    if need.get(key, 0) < o["dprev"]:
                        need[key] = o["dprev"]
                for key, val in need.items():
                    if waited.get(key, 0) >= val:
                        continue
                    waited[key] = val
                    sem = dsems[key[1:]] if key[0] == "d" else sems[key[1:]]
                    e.wait_ge(sem, val)
                ins = o["fn"](e)
                if ins is None:
                    continue
                if o["dma"]:
                    ins.then_inc(dsems[o["dsem"]], 16)
                elif o["sig"]:
                    ins.then_inc(sems[(engname, o["phase"])], 1)

        @block.tensor
        def _(e):
            run("pe", e)

        @block.scalar
        def _(e):
            run("act", e)

        @block.vector
        def _(e):
            run("dve", e)

        @block.gpsimd
        def _(e):
            run("pool", e)

        @block.sync
        def _(e):
            run("sp", e)


def build_program(debug_stage=None):
    nc = bass.Bass("TRN2", target_bir_lowering=False)
    SKIP = set((debug_stage or '').split('+')[1:])
    P = Prog()
    es = ExitStack()

    def din(name, shape):
        return DBuf(name, nc.dram_tensor(name, list(shape), F32, kind="ExternalInput").ap().tensor)

    x_d = din("x", (NSEQ * S, D))
    relb_d = din("rel_bias", (32, 10))
    win_d = din("w_in", (DEPTH * D, INC))
    qng_d = din("q_norm_g", (DEPTH, 256))
    kvng_d = din("kv_norm_g", (DEPTH, 128))
    wuq_d = din("w_uq", (DEPTH * 256, 576))
    wukv_d = din("w_ukv", (DEPTH * 128, 768))
    dlam_d = din("diff_lambda", (DEPTH, 128))
    subg_d = din("subln_g", (DEPTH, 64))
    wo_d = din("w_o", (DEPTH * 1024, 1024))
    ln1g_d = din("ln1_g", (DEPTH, 1024))
    ln1b_d = din("ln1_b", (DEPTH, 1024))
    ln2g_d = din("ln2_g", (DEPTH, 1024))
    ln2b_d = din("ln2_b", (DEPTH, 1024))
    wg_d = din("w_gate", (DEPTH * D, FF))
    wu_d = din("w_up", (DEPTH * D, FF))
    wd_d = din("w_down", (DEPTH * FF, D))
    cst_d = din("cst", (4 * 128, 128))
    cs_d = din("cs", (2 * 32, S))
    out_d = DBuf("out", nc.dram_tensor("out", [NSEQ * S, D], F32, kind="ExternalOutput").ap().tensor, track=True)
    if 'nointernal' not in SKIP:
        gsc_d = DBuf("gsc", nc.dram_tensor("gsc", [10, 2176 + 3 * 384], F32, kind="Internal").ap().tensor, track=True)
        ebb_d = DBuf("ebbd", nc.dram_tensor("ebbd", [4 * 128, 2048], BF16, kind="Internal").ap().tensor, track=True)
        eba_d = DBuf("ebad", nc.dram_tensor("ebad", [6 * 128, 768], BF16, kind="Internal").ap().tensor, track=True)

    WSC_N = 26 * 1024 * 1024
    wsc_d = DBuf("wsc", nc.dram_tensor("wsc", [WSC_N // 2048, 2048], BF16, kind="Internal").ap().tensor, track=True)
    wcache = {}
    wsc_off = [0]

    def wload(key, dst, src, n1, n2):
        pat = [[n1 * n2, 128], [n2, n1], [1, n2]]
        if key in wcache:
            P.op("sp", lambda e, o=dst, i=wsc_d.v(wcache[key], pat): e.dma_start(out=o.ap, in_=i.ap), [wsc_d.v(wcache[key], pat)], [dst], dma=True)
        else:
            P.op("pool", lambda e, o=dst, i=src: e.dma_start(out=o.ap, in_=i.ap), [src], [dst], dma=True)
            off = wsc_off[0]
            wsc_off[0] += 128 * n1 * n2
            assert wsc_off[0] <= WSC_N
            wcache[key] = off
            P.op("sp", lambda e, o=wsc_d.v(off, pat), i=dst: e.dma_start(out=o.ap, in_=i.ap), [dst], [wsc_d.v(off, pat)], dma=True)

    def sb(name, F, dt):
        t = es.enter_context(nc.sbuf_tensor(name, [128, F], dt))
        return Buf(name, t, 128, F)

    xf = sb("xf", 8 * S, F32)
    xb = sb("xb", 8 * S, BF16)
    AB = sb("arena_bf", 40448, BF16)
    AF_ = sb("arena_f32", 5120, F32)
    cst = sb("cst_s", 512, F32)
    cstb = sb("cstb", 256, BF16)
    prm = sb("prm", 96, F32)
    ps = []
    for i in range(8):
        t = es.enter_context(nc.psum_tensor(f"ps{i}", [128, 512], F32))
        ps.append(Buf(f"ps{i}", t, 128, 512))

    ident = cst.v(0, 128, 0, 128)
    ones_f = lambda p0, p1, n=128: cst.v(p0, p1, 128, 128 + n)
    antiid_b = cstb.v(0, 128, 0, 128)
    tri_b = cstb.v(0, 128, 128, 256)

    def mm(out, lhsT, rhs, start=True, stop=True):
        rd = [lhsT, rhs] + ([] if start else [out])
        P.op("pe", lambda e: e.matmul(out.ap, lhsT=lhsT.ap, rhs=rhs.ap, start=start, stop=stop), rd, [out])

    def tr(out, in_):
        if 'trmm' in SKIP:
            P.op("pe", lambda e: e.matmul(out.ap, lhsT=in_.ap, rhs=ident.ap, start=True, stop=True), [in_, ident], [out])
        else:
            P.op("pe", lambda e: e.transpose(out.ap, in_.ap, ident.ap), [in_, ident], [out])

    def act(out, in_, func, scale=1.0, bias=0.0, eng="act"):
        rd = [in_]
        b = bias
        if isinstance(bias, View):
            rd.append(bias); b = bias.ap
        s_ = scale
        if isinstance(scale, View):
            rd.append(scale); s_ = scale.ap
        P.op("act", lambda e: e.activation(out.ap, in_.ap, func, bias=b, scale=s_), rd, [out])

    def _veng(eng):
        return eng

    def tt(out, in0, in1, op, eng="dve"):
        P.op(eng, lambda e: e.tensor_tensor(out.ap, in0.ap, in1.ap, op), [in0, in1], [out])

    def ts(out, in0, s1, s2, op0, op1=None, eng="dve"):
        rd = [in0]
        a1 = s1
        if isinstance(s1, View):
            rd.append(s1); a1 = s1.ap
        a2 = s2
        if isinstance(s2, View):
            rd.append(s2); a2 = s2.ap
        if op1 is None:
            P.op(eng, lambda e: e.tensor_scalar(out.ap, in0.ap, a1, None, op0), rd, [out])
        else:
            P.op(eng, lambda e: e.tensor_scalar(out.ap, in0.ap, a1, a2, op0, op1), rd, [out])

    def stt(out, in0, sc, in1, op0, op1):
        rd = [in0, in1]
        a = sc
        if isinstance(sc, View):
            rd.append(sc); a = sc.ap
        P.op("dve", lambda e: e.scalar_tensor_tensor(out.ap, in0.ap, a, in1.ap, op0, op1), rd, [out])

    def cp(out, in_, eng="dve"):
        if eng == "act":
            P.op("act", lambda e: e.copy(out.ap, in_.ap), [in_], [out])
        else:
            P.op(eng, lambda e: e.tensor_copy(out.ap, in_.ap), [in_], [out])

    def recip(out, in_):
        P.op("dve", lambda e: e.reciprocal(out.ap, in_.ap), [in_], [out])

    def memset(out, val, eng="dve"):
        P.op(eng, lambda e: e.memset(out.ap, val), [], [out])

    def dma(q, out, in_, nonc=False):
        if nonc:
            P.op(q, lambda e: e.dma_start(out=out.ap, in_=in_.ap, allow_slow_non_contiguous=True), [in_], [out], dma=True)
        else:
            P.op(q, lambda e: e.dma_start(out=out.ap, in_=in_.ap), [in_], [out], dma=True)

    o = 0
    def carve(n):
        nonlocal o
        r = o
        o += n
        return r
    HT0 = carve(3 * S)
    QK0 = [carve(2 * S), carve(2 * S)]
    VA0 = [carve(16 * 256), carve(16 * 256)]
    PT0 = carve(4 * 512)
    WIN0 = [carve(8 * 384), carve(8 * 384)]
    WO0 = carve(3 * 1024)
    WC0 = carve(2 * 576 + 2 * 576 + 384 + 384)
    EBB0 = carve(2048)
    EBA0 = carve(2 * 768)
    assert o <= 40448, o
    H20 = 0
    GU0 = H20 + NFC * 1024
    DW0 = GU0 + 4 * 8 * 256
    assert DW0 + 3 * 1024 <= 40448

    PR_LN = 0
    PR_QG = 64
    PR_KVG = 68
    PR_SUB = 70
    PR_NLAM = 72
    PR_TMP = 74

    dma("sp", cst.v3(0, 128, 0, 4, 128, 128), cst_d.v(0, [[128, 128], [128 * 128, 4], [1, 128]]))
    cp(cstb.v(0, 128, 0, 256), cst.v(0, 128, 256, 512))
    if 'noprm' not in SKIP:
        for l in range(DEPTH):
            for j, dd in enumerate((ln1g_d, ln1b_d, ln2g_d, ln2b_d)):
                dma("sp", prm.v(0, 128, PR_LN + 32 * l + 8 * j, PR_LN + 32 * l + 8 * j + 8),
                    dd.v(l * 1024, [[1, 128], [128, 8]]), nonc=True)
            dma("sp", prm.v(0, 128, PR_QG + 2 * l, PR_QG + 2 * l + 2), qng_d.v(l * 256, [[1, 128], [128, 2]]), nonc=True)
            dma("sp", prm.v(0, 128, PR_KVG + l, PR_KVG + l + 1), kvng_d.v(l * 128, [[1, 128], [1, 1]]), nonc=True)
            for hh in range(2):
                dma("sp", prm.v(64 * hh, 64 * hh + 64, PR_SUB + l, PR_SUB + l + 1), subg_d.v(l * 64, [[1, 64], [1, 1]]), nonc=True)
            lam_init = 0.8 - 0.6 * math.exp(-0.3 * l)
            ts(prm.v(0, 128, PR_SUB + l, PR_SUB + l + 1), prm.v(0, 128, PR_SUB + l, PR_SUB + l + 1), 1.0 - lam_init, None, ALU.mult)
            dl = AF_.v(0, 128, 0, 128)
            dma("sp", dl, dlam_d.v(l * 128, [[0, 128], [1, 128]]), nonc=True)
            pr = AF_.v(0, 128, 128, 192)
            tt(AF_.v(0, 128, 128, 160), AF_.v(0, 128, 0, 32), AF_.v(0, 128, 32, 64), ALU.mult)
            tt(AF_.v(0, 128, 160, 192), AF_.v(0, 128, 64, 96), AF_.v(0, 128, 96, 128), ALU.mult)
            s12 = prm.v(0, 128, PR_TMP, PR_TMP + 2)
            P.op("dve", lambda e, a=AF_.v3(0, 128, 128, 2, 32, 32), b=s12: e.tensor_reduce(b.ap, a.ap, mybir.AxisListType.X, ALU.add),
                 [pr], [s12])
            e12 = prm.v(0, 128, PR_TMP + 2, PR_TMP + 4)
            act(e12, s12, AF.Exp)
            tt(prm.v(0, 128, PR_NLAM + l, PR_NLAM + l + 1), prm.v(0, 128, PR_TMP + 3, PR_TMP + 4), prm.v(0, 128, PR_TMP + 2, PR_TMP + 3), ALU.subtract)
            ts(prm.v(0, 128, PR_NLAM + l, PR_NLAM + l + 1), prm.v(0, 128, PR_NLAM + l, PR_NLAM + l + 1), -lam_init, None, ALU.add)

    if 'nobias' not in SKIP:
        GW = 2176 + 3 * 384
        gv = AF_.v(0, 10, 256, 256 + GW)
        tabA = AF_.v(0, 6, 4096, 4128)
        tabB = AF_.v(0, 4, 4128, 4160)
        dma("sp", tabA, relb_d.v(0, [[1, 6], [10, 32]]), nonc=True)
        dma("sp", tabB, relb_d.v(6, [[1, 4], [10, 32]]), nonc=True)
        memset(AF_.v(0, 10, 256, 256 + GW), 0.0)
        zer = AF_.v(0, 10, 4160, 4160 + 640)
        memset(zer, 0.0)
        gB = AF_.v(0, 4, 256, 256 + 2176)
        for (b, r0, r1) in bucket_ranges(2047, 1, True):
            j0, j1 = r0 + 127, min(r1 + 127, 2176)
            jj = j0
            while jj < j1:
                je = min(jj + 640, j1)
                act(AF_.v(0, 4, 256 + jj, 256 + je), AF_.v(0, 4, 4160, 4160 + je - jj), AF.Exp, bias=AF_.v(0, 4, 4128 + b, 4128 + b + 1))
                jj = je
        dma("sp", gsc_d.v(0, [[GW, 4], [1, 2176]]), gB)
        gA0 = 256 + 2176
        for li, (win_, dil) in enumerate(A_PAIRS):
            for (b, r0, r1) in bucket_ranges(128, dil):
                j0, j1 = r0 + 127, r1 + 127
                act(AF_.v(0, 6, gA0 + li * 384 + j0, gA0 + li * 384 + j1), AF_.v(0, 6, 4160, 4160 + j1 - j0), AF.Exp,
                    bias=AF_.v(0, 6, 4096 + b, 4096 + b + 1))
        dma("sp", gsc_d.v(4 * GW + 2176, [[GW, 6], [1, 3 * 384]]), AF_.v(0, 6, gA0, gA0 + 3 * 384))
        for h in range(4):
            hk = AB.v(0, 128, 0, 2048)
            for half in range(2):
                dma("pool", AB.v(0, 128, half * 1024, half * 1024 + 1024), gsc_d.v(h * GW + half * 1024, [[1, 128], [1, 1024]]))
            st_ = AB.v(0, 128, 2048, 4096)
            for g in range(4):
                mm(ps[g].v(0, 128, 0, 512), antiid_b, AB.v(0, 128, g * 512, g * 512 + 512))
                cp(AB.v(0, 128, 2048 + g * 512, 2048 + g * 512 + 512), ps[g].v(0, 128, 0, 512), eng=("act" if g % 2 else "dve"))
            dma("sp", ebb_d.v(h * 128 * 2048, [[2048, 128], [1, 2048]]), st_)
        for h in range(6):
            for li in range(3):
                dma("pool", AB.v(0, 128, 4096 + li * 256, 4096 + li * 256 + 256),
                    gsc_d.v((4 + h) * GW + 2176 + li * 384, [[1, 128], [1, 256]]))
            mm(ps[4 + h % 2].v(0, 128, 0, 512), antiid_b, AB.v(0, 128, 4096, 4096 + 512))
            mm(ps[6 + h % 2].v(0, 128, 0, 256), antiid_b, AB.v(0, 128, 4096 + 512, 4096 + 768))
            cp(AB.v(0, 128, 5120, 5120 + 512), ps[4 + h % 2].v(0, 128, 0, 512))
            cp(AB.v(0, 128, 5120 + 512, 5120 + 768), ps[6 + h % 2].v(0, 128, 0, 256), eng="act")
            dma("sp", eba_d.v(h * 128 * 768, [[768, 128], [1, 768]]), AB.v(0, 128, 5120, 5120 + 768))

    def set_vaug_ones():
        for bi in range(2):
            memset(AB.v3(0, 128, VA0[bi] + 64, 32, 128, 64), 1.0, eng="pool")

    def xbv(c, t0, t1, step=1):
        return xb.v(0, 128, c * S + t0, c * S + t1, step)

    def load_win_cols(l, dst_off, col0, ncols, dst_stride, dst_col0=0):
        dstv = AB.v3(0, 128, dst_off + dst_col0, 8, dst_stride, ncols)
        srcv = win_d.v(l * D * INC + col0, [[INC, 128], [128 * INC, 8], [1, ncols]])
        wload(("win", l, col0, ncols), dstv, srcv, 8, ncols)

    evq = [0]

    def evac(out, in_):
        evq[0] += 1
        cp(out, in_, eng=("act" if evq[0] % 2 else "dve"))

    def inproj(wbase, wstride, wcol0, M, outbuf_fn, psbanks):
        for g in range(4):
            pb = ps[psbanks[g % len(psbanks)]]
            for c in range(8):
                mm(pb.v(0, M, 0, 512), AB.v(0, 128, wbase + c * wstride + wcol0, wbase + c * wstride + wcol0 + M),
                   xbv(c, g * 512, g * 512 + 512), start=(c == 0), stop=(c == 7))
            outbuf_fn(g, pb)

    def build_vaug(vabuf, wbase, wstride, wcol0, tok_fn, unit0=0):
        for blk in range(16):
            pb = ps[2 + blk % 2]
            t0, t1, st = tok_fn(blk)
            for c in range(8):
                mm(pb.v(0, 128, 0, 128), xbv(c, t0, t1, st),
                   AB.v(0, 128, wbase + c * wstride + wcol0, wbase + c * wstride + wcol0 + 128), start=(c == 0), stop=(c == 7))
            evac(AB.v3(0, 128, vabuf + blk * 256 + unit0 * 128, 2, 128, 64), pb.v3(0, 128, 0, 2, 64, 64))

    pti = [0]

    PT_BC = [PT0 + i * 512 for i in range(4)] + [EBA0 + i * 512 for i in range(3)]
    PT_A = [PT0 + i * 256 for i in range(8)]

    def next_pt(pool=None):
        pool = pool or PT_BC
        pti[0] += 1
        return pool[pti[0] % len(pool)]

    LOOK = 3

    def pipeline(s1, s2, L=LOOK):
        n = len(s1)
        for i in range(n + L):
            if i < n:
                s1[i]()
            if i - L >= 0:
                s2[i - L]()

    def stage_dump_and_finish(seq):
        store_out(seq)

    def store_out(seq):
        for t in range(16):
            stg = AF_.v(0, 128, (t % 4) * 1024, (t % 4) * 1024 + 1024)
            for half in range(2):
                pb = ps[(2 * t + half) % 8]
                for cc in range(4):
                    c = half * 4 + cc
                    tr(pb.v(0, 128, cc * 128, cc * 128 + 128), xf.v(0, 128, c * S + t * 128, c * S + t * 128 + 128))
                cp(AF_.v(0, 128, (t % 4) * 1024 + half * 512, (t % 4) * 1024 + half * 512 + 512), pb.v(0, 128, 0, 512),
                   eng=("act" if half else "dve"))
            dma("sp", out_d.v((seq * S + t * 128) * D, [[D, 128], [1, D]]), stg)

    def load_x(seq):
        for tg in range(4):
            for tt_ in range(4):
                t = tg * 4 + tt_
                dma("sp", AF_.v(0, 128, tt_ * 1024, tt_ * 1024 + 1024), x_d.v((seq * S + t * 128) * D, [[D, 128], [1, D]]))
            for c in range(8):
                pb = ps[c % 8]
                for tt_ in range(4):
                    tr(pb.v(0, 128, tt_ * 128, tt_ * 128 + 128), AF_.v(0, 128, tt_ * 1024 + c * 128, tt_ * 1024 + c * 128 + 128))
                cp(xf.v(0, 128, c * S + tg * 512, c * S + tg * 512 + 512), pb.v(0, 128, 0, 512), eng="dve")
                cp(xb.v(0, 128, c * S + tg * 512, c * S + tg * 512 + 512), xf.v(0, 128, c * S + tg * 512, c * S + tg * 512 + 512), eng="act")

    def wo_pass(l, row0, npairs, first):
        for pi in range(npairs):
            for hf in range(2):
                wload(("wo", l, row0 + pi * 128, hf), AB.v3(0, 128, WO0 + pi * 1024 + hf * 512, 1, 512, 512),
                      wo_d.v((l * 1024 + row0 + pi * 128) * 1024 + hf * 512, [[1024, 128], [512, 1], [1, 512]]), 1, 512)
        k = 0
        for c in range(8):
            for g in range(4):
                pb = ps[k % 4]
                k += 1
                for pi in range(npairs):
                    mm(pb.v(0, 128, 0, 512), AB.v(0, 128, WO0 + pi * 1024 + c * 128, WO0 + pi * 1024 + c * 128 + 128),
                       AB.v(0, 128, HT0 + pi * S + g * 512, HT0 + pi * S + g * 512 + 512), start=(pi == 0), stop=(pi == npairs - 1))
                xv = xf.v(0, 128, c * S + g * 512, c * S + g * 512 + 512)
                if first:
                    stt(xv, xv, ALPHA, pb.v(0, 128, 0, 512), ALU.mult, ALU.add)
                else:
                    tt(xv, xv, pb.v(0, 128, 0, 512), ALU.add)

    def layer_norm(l, which):
        gcol = PR_LN + 32 * l + (0 if which == 1 else 16)
        bcol = gcol + 8
        for g in range(4):
            s1 = ps[(2 * g) % 4]
            s2 = ps[(2 * g + 1) % 4]
            for c in range(8):
                yv = xf.v(0, 128, c * S + g * 512, c * S + g * 512 + 512)
                sq = AF_.v(0, 128, (c % 2) * 512, (c % 2) * 512 + 512)
                act(sq, yv, AF.Square)
                mm(s1.v(0, 128, 0, 512), ones_f(0, 128), yv, start=(c == 0), stop=(c == 7))
                mm(s2.v(0, 128, 0, 512), ones_f(0, 128), sq, start=(c == 0), stop=(c == 7))
            mean = AF_.v(0, 128, 1024, 1536)
            ts(mean, s1.v(0, 128, 0, 512), 1.0 / D, None, ALU.mult)
            msq = AF_.v(0, 128, 1536, 2048)
            tt(msq, mean, mean, ALU.mult)
            var = AF_.v(0, 128, 2048, 2560)
            stt(var, s2.v(0, 128, 0, 512), 1.0 / D, msq, ALU.mult, ALU.subtract)
            lnv = AF_.v(0, 128, 2560, 3072)
            act(lnv, var, AF.Ln, bias=prm_eps(LN_EPS))
            rstd = ps[4 + g % 2].v(0, 128, 0, 512)
            act(rstd, lnv, AF.Exp, scale=-0.5)
            mr = ps[6 + g % 2].v(0, 128, 0, 512)
            tt(mr, mean, rstd, ALU.mult)
            for c in range(8):
                yv = xf.v(0, 128, c * S + g * 512, c * S + g * 512 + 512)
                t1 = AF_.v(0, 128, 3072 + (c % 2) * 1024, 3072 + (c % 2) * 1024 + 512)
                tt(t1, yv, rstd, ALU.mult)
                t2 = AF_.v(0, 128, 3072 + (c % 2) * 1024 + 512, 3072 + (c % 2) * 1024 + 1024)
                tt(t2, t1, mr, ALU.subtract)
                act(yv, t2, AF.Identity, scale=prm.v(0, 128, gcol + c, gcol + c + 1), bias=prm.v(0, 128, bcol + c, bcol + c + 1))
                ts(xb.v(0, 128, c * S + g * 512, c * S + g * 512 + 512), t2, prm.v(0, 128, gcol + c, gcol + c + 1),
                   prm.v(0, 128, bcol + c, bcol + c + 1), ALU.mult, ALU.add, eng="pool")

    epsv = {}

    def prm_eps(val, p0=0, p1=128):
        if val not in epsv:
            col = 80 + len(epsv)
            memset(prm.v(0, 128, col, col + 1), val)
            epsv[val] = col
        col = epsv[val]
        return prm.v(p0, p1, col, col + 1)

    def ffn(l):
        for tg in range(2):
            T0 = tg * 1024
            gi = 0
            for fb in range(11):
                gb = GU0 + (fb % 2) * 2 * 2048
                for wi, wd in enumerate((wg_d, wu_d)):
                    wload(("gu", l, wi, fb), AB.v3(0, 128, gb + wi * 2048, 8, 256, 256),
                          wd.v(l * D * FF + fb * 256, [[FF, 128], [128 * FF, 8], [1, 256]]), 8, 256)
                for fc in range(2):
                    f = fb * 2 + fc
                    for half in range(2):
                        pg = ps[(gi % 2) * 2]
                        pu = ps[(gi % 2) * 2 + 1]
                        gi += 1
                        for c in range(8):
                            mm(pg.v(0, 128, 0, 512), AB.v(0, 128, gb + c * 256 + fc * 128, gb + c * 256 + fc * 128 + 128),
                               xbv(c, T0 + half * 512, T0 + half * 512 + 512), start=(c == 0), stop=(c == 7))
                        for c in range(8):
                            mm(pu.v(0, 128, 0, 512), AB.v(0, 128, gb + 2048 + c * 256 + fc * 128, gb + 2048 + c * 256 + fc * 128 + 128),
                               xbv(c, T0 + half * 512, T0 + half * 512 + 512), start=(c == 0), stop=(c == 7))
                        sg = AF_.v(0, 128, (gi % 2) * 512, (gi % 2) * 512 + 512)
                        act(sg, pg.v(0, 128, 0, 512), AF.Silu)
                        tt(AB.v(0, 128, H20 + f * 1024 + half * 512, H20 + f * 1024 + half * 512 + 512), sg, pu.v(0, 128, 0, 512), ALU.mult)
            di = 0
            for dp in range(4):
                for f2 in range(11):
                    db = DW0 + (di % 3) * 1024
                    di += 1
                    wload(("dw", l, dp, f2), AB.v3(0, 128, db, 2, 512, 256),
                          wd_d.v((l * FF + f2 * 256) * D + dp * 256, [[D, 128], [128 * D, 2], [1, 256]]), 2, 256)
                    for fc in range(2):
                        f = f2 * 2 + fc
                        for cc in range(2):
                            for half in range(2):
                                mm(ps[4 + cc * 2 + half].v(0, 128, 0, 512), AB.v(0, 128, db + fc * 512 + cc * 128, db + fc * 512 + cc * 128 + 128),
                                   AB.v(0, 128, H20 + f * 1024 + half * 512, H20 + f * 1024 + half * 512 + 512),
                                   start=(f == 0), stop=(f == NFC - 1))
                for cc in range(2):
                    c = dp * 2 + cc
                    for half in range(2):
                        xv = xf.v(0, 128, c * S + T0 + half * 512, c * S + T0 + half * 512 + 512)
                        stt(xv, xv, ALPHA, ps[4 + cc * 2 + half].v(0, 128, 0, 512), ALU.mult, ALU.add)

    vac = [0]

    def attn_A(l, j):
        wb_ = WIN0[j % 2]
        qk = QK0[j % 2]
        load_win_cols(l, wb_, 128 * j, 128, 384, 0)
        load_win_cols(l, wb_, 384 + 128 * j, 128, 384, 128)
        load_win_cols(l, wb_, 768 + 128 * j, 128, 384, 256)
        dma("sp", AB.v3(0, 128, EBA0, 2, 768, 768), eba_d.v(2 * j * 128 * 768, [[768, 128], [128 * 768, 2], [1, 768]]))
        inproj(wb_, 384, 0, 128, lambda g, pb: evac(AB.v(0, 128, qk + g * 512, qk + g * 512 + 512), pb.v(0, 128, 0, 512)), (0, 1))
        inproj(wb_, 384, 128, 128, lambda g, pb: evac(AB.v(0, 128, qk + S + g * 512, qk + S + g * 512 + 512), pb.v(0, 128, 0, 512)), (0, 1))
        for li in (2, 0, 1):
            win_, dil = A_PAIRS[li]
            ncls = dil
            nb = 16 // dil

            def tok(blk):
                r, m = blk // nb, blk % nb
                t0 = r + dil * 128 * m
                return t0, t0 + dil * 127 + 1, dil
            vac[0] += 1
            va = VA0[vac[0] % 2]
            build_vaug(va, wb_, 384, 256, tok)
            for e in range(2):
                acc = AF_
                s1, s2 = [], []
                for blk in range(16):
                    st = {}

                    def f1(blk=blk, st=st):
                        r, m = blk // nb, blk % nb
                        nq = 256 if m < nb - 1 else 128
                        t0 = r + dil * 128 * m
                        kT = AB.v(64 * e, 64 * e + 64, qk + S + t0, qk + S + t0 + dil * 127 + 1, dil)
                        qT = AB.v(64 * e, 64 * e + 64, qk + t0, qk + t0 + dil * (nq - 1) + 1, dil)
                        pS = ps[(4, 5, 2, 3)[blk % 4]]
                        mm(pS.v(0, 128, 0, nq), kT, qT)
                        pt = next_pt(PT_A)
                        st["pt"] = pt
                        act(AB.v(0, 128, pt, pt + nq), pS.v(0, 128, 0, nq), AF.Exp, scale=0.125)
                        tt(AB.v(0, 128, pt, pt + nq), AB.v(0, 128, pt, pt + nq),
                           AB.v(0, 128, EBA0 + e * 768 + li * 256, EBA0 + e * 768 + li * 256 + nq), ALU.mult)

                    def f2(blk=blk, st=st):
                        r, m = blk // nb, blk % nb
                        nq = 256 if m < nb - 1 else 128
                        t0 = r + dil * 128 * m
                        pt = st["pt"]
                        pO = ps[6 + blk % 2]
                        mm(pO.v(0, 128, 0, nq), AB.v(0, 128, va + blk * 256 + 64 * e, va + blk * 256 + 64 * e + 128), AB.v(0, 128, pt, pt + nq))
                        av = AF_.v(0, 128, e * S + t0, e * S + t0 + dil * (nq - 1) + 1, dil)
                        if li == 2:
                            cp(av, pO.v(0, 128, 0, nq))
                        else:
                            tt(av, av, pO.v(0, 128, 0, nq), ALU.add)
                    s1.append(f1)
                    s2.append(f2)
                pipeline(s1, s2)
        for e in range(2):
            orow, lrow = (0, 64) if e == 0 else (64, 0)
            for g in range(4):
                rc = ps[g % 2].v(orow, orow + 64, 0, 512)
                recip(rc, AF_.v(lrow, lrow + 64, e * S + g * 512, e * S + g * 512 + 512))
                tt(AB.v(orow, orow + 64, HT0 + j * S + g * 512, HT0 + j * S + g * 512 + 512),
                   AF_.v(orow, orow + 64, e * S + g * 512, e * S + g * 512 + 512), rc, ALU.mult)

    def causal_group(g, scale, qv_fn, kv_fn, va, vaoff, strip_off, pO, psS=(4, 5, 0, 1)):
        nkb = 4 * g + 4
        s1, s2 = [], []
        for kb in range(nkb):
            jd = kb - 4 * g
            q0 = g * 512 + (128 * jd if jd > 0 else 0)
            nq = g * 512 + 512 - q0
            st = {}

            def f1(kb=kb, jd=jd, q0=q0, nq=nq, st=st):
                pS = ps[psS[kb % len(psS)]]
                mm(pS.v(0, 128, 0, nq), kv_fn(kb * 128, kb * 128 + 128), qv_fn(q0, q0 + nq))
                pt = next_pt()
                st["pt"] = pt
                act(AB.v(0, 128, pt, pt + nq), pS.v(0, 128, 0, nq), AF.Exp, scale=scale)
                if strip_off is not None:
                    so = strip_off + q0 - 128 * kb
                    tt(AB.v(0, 128, pt, pt + nq), AB.v(0, 128, pt, pt + nq), AB.v(0, 128, so, so + nq), ALU.mult)
                elif jd >= 0:
                    tt(AB.v(0, 128, pt, pt + 128), AB.v(0, 128, pt, pt + 128), tri_b, ALU.mult)

            def f2(kb=kb, q0=q0, nq=nq, st=st):
                pt = st["pt"]
                mm(pO.v(0, 128, q0 - g * 512, q0 - g * 512 + nq), AB.v(0, 128, va + kb * 256 + vaoff, va + kb * 256 + vaoff + 128),
                   AB.v(0, 128, pt, pt + nq), start=(kb == 0), stop=(kb == nkb - 1))
            s1.append(f1)
            s2.append(f2)
        pipeline(s1, s2)

    def attn_B(l, p):
        wb_ = WIN0[p % 2]
        va = VA0[p % 2]
        b0 = 1152
        load_win_cols(l, wb_, b0 + 512 + 128 * p, 128, 384, 256)
        build_vaug(va, wb_, 384, 256, lambda blk: (blk * 128, blk * 128 + 128, 1))
        for e in range(2):
            h = 2 * p + e
            qk = QK0[h % 2]
            load_win_cols(l, wb_, b0 + 64 * h, 64, 384, 0 + 64 * e)
            load_win_cols(l, wb_, b0 + 256 + 64 * h, 64, 384, 128 + 64 * e)
            dma("sp", AB.v(0, 128, EBB0, EBB0 + 2048), ebb_d.v(h * 128 * 2048, [[2048, 128], [1, 2048]]))
            inproj(wb_, 384, 64 * e, 64, lambda g, pb: evac(AB.v(0, 64, qk + g * 512, qk + g * 512 + 512), pb.v(0, 64, 0, 512)), (0, 1))
            inproj(wb_, 384, 128 + 64 * e, 64, lambda g, pb: evac(AB.v(0, 64, qk + S + g * 512, qk + S + g * 512 + 512), pb.v(0, 64, 0, 512)), (0, 1))
            orow, lrow = (0, 64) if e == 0 else (64, 0)
            prev_epi = None
            for g in range(4):
                pOs = [ps[(6 if g % 2 == 0 else 2) + m] for m in range(2)]
                for m in range(2):
                    causal_group(g, 32 ** -0.5,
                                 lambda a, b, m=m: AB.v(32 * m, 32 * m + 32, qk + a, qk + b),
                                 lambda a, b, m=m: AB.v(32 * m, 32 * m + 32, qk + S + a, qk + S + b),
                                 va, 64 * e, EBB0, pOs[m])

                def epi(g=g, pOs=pOs):
                    ons = []
                    for m in range(2):
                        pO = pOs[m]
                        sl = (m * 2 + g % 2) * 512
                        rcv = AF_.v(orow, orow + 64, sl, sl + 512)
                        recip(rcv, pO.v(lrow, lrow + 64, 0, 512))
                        on = AF_.v(orow, orow + 64, 2048 + sl, 2048 + sl + 512)
                        tt(on, pO.v(orow, orow + 64, 0, 512), rcv, ALU.mult)
                        ons.append(on)
                    ob = ons[1]
                    stt(ob, ons[1], prm.v(orow, orow + 64, PR_NLAM + l, PR_NLAM + l + 1), ons[0], ALU.mult, ALU.add)
                    sq = AF_.v(orow, orow + 64, 4096 + (g % 2) * 512, 4096 + (g % 2) * 512 + 512)
                    act(sq, ob, AF.Square)
                    pm = ps[g % 2]
                    mm(pm.v(0, 128, 0, 512), ones_f(orow, orow + 64), sq)
                    act(sq, pm.v(orow, orow + 64, 0, 512), AF.Ln, scale=1.0 / 64, bias=prm_eps(1e-5, orow, orow + 64))
                    act(sq, sq, AF.Exp, scale=-0.5)
                    stt(AB.v(orow, orow + 64, HT0 + p * S + g * 512, HT0 + p * S + g * 512 + 512), ob,
                        prm.v(orow, orow + 64, PR_SUB + l, PR_SUB + l + 1), sq, ALU.mult, ALU.mult)
                if prev_epi is not None:
                    prev_epi()
                prev_epi = epi
            prev_epi()

    def attn_C(l):
        c0 = 1920
        lrow = l * 256
        WUQ = WC0
        WUQR = WC0 + 1152
        WKK = WC0 + 2304
        WKV = WC0 + 2304 + 384
        memset(AB.v(0, 128, WUQR, WUQR + 1152), 0.0, eng="pool")
        for c in range(2):
            dma("pool", AB.v(0, 128, WUQ + c * 576, WUQ + c * 576 + 576), wuq_d.v((l * 256 + c * 128) * 576, [[576, 128], [1, 576]]))
            for hf in range(2):
                dma("pool", AB.v3(0, 128, WUQR + c * 576 + 64 + 16 * hf, 6, 96, 16),
                    wuq_d.v((l * 256 + c * 128) * 576 + 64 + 16 * (1 - hf), [[576, 128], [96, 6], [1, 16]]))
        dma("pool", AB.v3(0, 128, WKK, 6, 64, 64), wukv_d.v(l * 128 * 768, [[768, 128], [128, 6], [1, 64]]))
        dma("pool", AB.v3(0, 128, WKV, 6, 64, 64), wukv_d.v(l * 128 * 768 + 64, [[768, 128], [128, 6], [1, 64]]))
        dma("sp", AF_.v(64, 96, 0, S), cs_d.v(0, [[S, 32], [1, S]]))
        dma("sp", AF_.v(64, 96, S, 2 * S), cs_d.v(32 * S, [[S, 32], [1, S]]))
        w0, w1 = WIN0[0], WIN0[1]
        load_win_cols(l, w0, c0, 256, 384, 0)
        memset(AB.v(0, 128, w1, w1 + 8 * 384), 0.0, eng="pool")
        load_win_cols(l, w1, c0 + 256, 128, 384, 0)
        load_win_cols(l, w1, c0 + 384, 32, 384, 128 + 64)
        load_win_cols(l, w1, c0 + 384 + 16, 16, 384, 224 + 64)
        load_win_cols(l, w1, c0 + 384, 16, 384, 224 + 64 + 16)
        CQN = QK0[0]
        CKVN = QK0[1]
        KC = [WO0, WO0]
        QH = [QK0[1] + S, EBB0]
        for g in range(4):
            tsl = (g * 512, g * 512 + 512)
            for ch in range(2):
                for c in range(8):
                    mm(ps[ch].v(0, 128, 0, 512), AB.v(0, 128, w0 + c * 384 + ch * 128, w0 + c * 384 + ch * 128 + 128),
                       xbv(c, *tsl), start=(c == 0), stop=(c == 7))
                sq = AF_.v(0, 128, 4096 + ch * 512, 4096 + ch * 512 + 512)
                act(sq, ps[ch].v(0, 128, 0, 512), AF.Square)
                mm(ps[2].v(0, 128, 0, 512), ones_f(0, 128), sq, start=(ch == 0), stop=(ch == 1))
            rs = AF_.v(0, 128, 4096, 4608)
            act(rs, ps[2].v(0, 128, 0, 512), AF.Ln, scale=1.0 / 256, bias=prm_eps(1e-6))
            act(rs, rs, AF.Exp, scale=-0.5)
            for ch in range(2):
                stt(AB.v(0, 128, CQN + ch * S + tsl[0], CQN + ch * S + tsl[1]), ps[ch].v(0, 128, 0, 512),
                    prm.v(0, 128, PR_QG + 2 * l + ch, PR_QG + 2 * l + ch + 1), rs, ALU.mult, ALU.mult)
            for c in range(8):
                mm(ps[3].v(0, 128, 0, 512), AB.v(0, 128, w1 + c * 384, w1 + c * 384 + 128), xbv(c, *tsl), start=(c == 0), stop=(c == 7))
            sq = AF_.v(0, 128, 4608, 5120)
            act(sq, ps[3].v(0, 128, 0, 512), AF.Square)
            mm(ps[2].v(0, 128, 0, 512), ones_f(0, 128), sq)
            act(sq, ps[2].v(0, 128, 0, 512), AF.Ln, scale=1.0 / 128, bias=prm_eps(1e-6))
            act(sq, sq, AF.Exp, scale=-0.5)
            stt(AB.v(0, 128, CKVN + tsl[0], CKVN + tsl[1]), ps[3].v(0, 128, 0, 512),
                prm.v(0, 128, PR_KVG + l, PR_KVG + l + 1), sq, ALU.mult, ALU.mult)
            for c in range(8):
                mm(ps[6].v(0, 96, 0, 512), AB.v(0, 128, w1 + c * 384 + 128, w1 + c * 384 + 224), xbv(c, *tsl), start=(c == 0), stop=(c == 7))
            for c in range(8):
                mm(ps[7].v(0, 96, 0, 512), AB.v(0, 128, w1 + c * 384 + 224, w1 + c * 384 + 320), xbv(c, *tsl), start=(c == 0), stop=(c == 7))
            t1 = AF_.v(64, 96, 4096, 4608)
            tt(t1, ps[6].v(64, 96, 0, 512), AF_.v(64, 96, tsl[0], tsl[1]), ALU.mult)
            t2 = AF_.v(64, 96, 4608, 5120)
            tt(t2, ps[7].v(64, 96, 0, 512), AF_.v(64, 96, S + tsl[0], S + tsl[1]), ALU.mult)
            tt(AB.v(64, 96, KC[0] + tsl[0], KC[0] + tsl[1]), t1, t2, ALU.add)
        for h in range(6):
            pp, e = h // 2, h % 2
            va = VA0[pp % 2]
            if e == 0:
                for blk in range(16):
                    pb = ps[2 + blk % 2]
                    mm(pb.v(0, 128, 0, 128), AB.v(0, 128, CKVN + blk * 128, CKVN + blk * 128 + 128),
                       AB.v(0, 128, WKV + pp * 128, WKV + pp * 128 + 128))
                    evac(AB.v3(0, 128, va + blk * 256, 2, 128, 64), pb.v3(0, 128, 0, 2, 64, 64))
            kc = KC[h % 2]
            qh = QH[h % 2]
            for g in range(4):
                tsl = (g * 512, g * 512 + 512)
                for ch in range(2):
                    mm(ps[0].v(0, 96, 0, 512), AB.v(0, 128, WUQ + ch * 576 + h * 96, WUQ + ch * 576 + h * 96 + 96),
                       AB.v(0, 128, CQN + ch * S + tsl[0], CQN + ch * S + tsl[1]), start=(ch == 0), stop=(ch == 1))
                for ch in range(2):
                    mm(ps[1].v(0, 96, 0, 512), AB.v(0, 128, WUQR + ch * 576 + h * 96, WUQR + ch * 576 + h * 96 + 96),
                       AB.v(0, 128, CQN + ch * S + tsl[0], CQN + ch * S + tsl[1]), start=(ch == 0), stop=(ch == 1))
                evac(AB.v(0, 64, qh + tsl[0], qh + tsl[1]), ps[0].v(0, 64, 0, 512))
                t1 = AF_.v(64, 96, 4096, 4608)
                tt(t1, ps[0].v(64, 96, 0, 512), AF_.v(64, 96, tsl[0], tsl[1]), ALU.mult)
                t2 = AF_.v(64, 96, 4608, 5120)
                tt(t2, ps[1].v(64, 96, 0, 512), AF_.v(64, 96, S + tsl[0], S + tsl[1]), ALU.mult)
                tt(AB.v(64, 96, qh + tsl[0], qh + tsl[1]), t1, t2, ALU.add)
                mm(ps[2 + g % 2].v(0, 64, 0, 512), AB.v(0, 128, WKK + h * 64, WKK + h * 64 + 64), AB.v(0, 128, CKVN + tsl[0], CKVN + tsl[1]))
                evac(AB.v(0, 64, kc + tsl[0], kc + tsl[1]), ps[2 + g % 2].v(0, 64, 0, 512))
            orow, lrow_ = (0, 64) if e == 0 else (64, 0)

            prev_epi = None
            for g in range(4):
                pO = ps[6 + g % 2]
                causal_group(g, 96 ** -0.5,
                             lambda a, b: AB.v(0, 96, qh + a, qh + b),
                             lambda a, b: AB.v(0, 96, kc + a, kc + b),
                             va, 64 * e, None, pO)

                def epi(g=g, pO=pO):
                    rcv = AF_.v(orow, orow + 64, 4096 + (g % 2) * 512, 4096 + (g % 2) * 512 + 512)
                    recip(rcv, pO.v(lrow_, lrow_ + 64, 0, 512))
                    tt(AB.v(orow, orow + 64, HT0 + pp * S + g * 512, HT0 + pp * S + g * 512 + 512), pO.v(orow, orow + 64, 0, 512), rcv, ALU.mult)
                if prev_epi is not None:
                    prev_epi()
                prev_epi = epi
            prev_epi()

    stages = (debug_stage or "full").split('+')[0]
    for seq in range(NSEQ):
        if stages != "full" and seq > 0:
            break
        if 'noload' not in SKIP:
            load_x(seq)
        for l in range(DEPTH):
            P.new_phase()
            if stages == "load":
                break
            set_vaug_ones()
            P.mark(f"s{seq}l{l}A")
            for j in range(3):
                attn_A(l, j)
            P.mark(f"s{seq}l{l}woA")
            wo_pass(l, 0, 3, True)
            if stages == "A":
                break
            P.mark(f"s{seq}l{l}B")
            for p in range(2):
                attn_B(l, p)
            P.mark(f"s{seq}l{l}woB")
            wo_pass(l, 384, 2, False)
            if stages == "B":
                break
            P.mark(f"s{seq}l{l}C")
            attn_C(l)
            P.mark(f"s{seq}l{l}woC")
            wo_pass(l, 640, 3, False)
            if stages == "C":
                break
            P.mark(f"s{seq}l{l}LN1")
            layer_norm(l, 1)
            if stages == "LN1":
                break
            P.mark(f"s{seq}l{l}FFN")
            ffn(l)
            P.mark(f"s{seq}l{l}LN2")
            if stages == "FFN":
                break
            layer_norm(l, 2)
            if stages == "L0":
                break
        if 'nostore' not in SKIP:
            store_out(seq)
    P.op("sp", lambda e: None, [out_d.v(0, [[1, NSEQ * S * D]])], [])
    P.emit(nc, es)
    es.close()
    nc._marks = P.marks
    return nc


_CACHE = {}


def host_consts():
    ident = np.eye(128, dtype=np.float32)
    ones = np.ones((128, 128), np.float32)
    anti = np.ascontiguousarray(ident[::-1])
    k = np.arange(128)[:, None]
    q = np.arange(128)[None, :]
    tri = (k <= q).astype(np.float32)
    cst = np.concatenate([ident, ones, anti, tri], axis=0)
    inv = (np.float32(10000.0) ** (-np.arange(16, dtype=np.float32) / np.float32(16))).astype(np.float32)
    ang = (np.arange(S, dtype=np.float32)[:, None] * inv[None, :]).astype(np.float32)
    cos = np.cos(ang).astype(np.float32).T
    sin = np.sin(ang).astype(np.float32).T
    cs = np.concatenate([cos, cos, -sin, sin], axis=0).astype(np.float32)
    return cst, cs


def kernel(x, rel_bias, w_in, q_norm_g, kv_norm_g, w_uq, w_ukv, diff_lambda, subln_g, w_o,
           ln1_g, ln1_b, ln2_g, ln2_b, w_gate, w_up, w_down):
    f = lambda a: np.ascontiguousarray(np.asarray(a, dtype=np.float32))
    key = DEBUG_STAGE or "full"
    if key not in _CACHE:
        _CACHE[key] = build_program(DEBUG_STAGE)
    nc = _CACHE[key]
    cst, cs = host_consts()
    shared = {
        "rel_bias": f(rel_bias), "w_in": f(w_in).reshape(DEPTH * D, INC), "q_norm_g": f(q_norm_g), "kv_norm_g": f(kv_norm_g),
        "w_uq": f(w_uq).reshape(DEPTH * 256, 576), "w_ukv": f(w_ukv).reshape(DEPTH * 128, 768),
        "diff_lambda": f(diff_lambda).reshape(DEPTH, 128), "subln_g": f(subln_g),
        "w_o": f(w_o).reshape(DEPTH * 1024, 1024), "ln1_g": f(ln1_g), "ln1_b": f(ln1_b), "ln2_g": f(ln2_g), "ln2_b": f(ln2_b),
        "w_gate": f(w_gate).reshape(DEPTH * D, FF), "w_up": f(w_up).reshape(DEPTH * D, FF), "w_down": f(w_down).reshape(DEPTH * FF, D),
        "cst": cst, "cs": cs,
    }
    xs = f(x)
    ncores = 1 if DEBUG_STAGE else 8
    in_maps = []
    for c in range(ncores):
        m = dict(shared)
        m["x"] = xs[NSEQ * c:NSEQ * (c + 1)].reshape(NSEQ * S, D)
        in_maps.append(m)
    res = run_bass_kernel_spmd(nc, in_maps, core_ids=list(range(ncores)))
    out = np.concatenate([np.asarray(r["out"]).reshape(NSEQ, S, D) for r in res.results], axis=0)
    return out.astype(np.float32)
```

```python
import math
from contextlib import ExitStack
import numpy as np
import concourse.bass as bass
import concourse.mybir as mybir
from concourse.bass_utils import run_bass_kernel_spmd

F32 = mybir.dt.float32
BF16 = mybir.dt.bfloat16
ALU = mybir.AluOpType
AF = mybir.ActivationFunctionType

S = 2048
D = 1024
NSEQ = 2
DEPTH = 2
INC = 2336
FF = 2816
NFC = 22
ALPHA = (2 * DEPTH) ** 0.25
LN_EPS = 1e-5
A_PAIRS = ((128, 1), (512, 4), (2048, 16))
DEBUG_STAGE = None
BUCKET_RINT = False


def t5_bucket_np(d, force_trunc=False):
    d = np.maximum(np.asarray(d, dtype=np.int64), 0)
    df = np.maximum(d, 1).astype(np.float32)
    val = np.log(df / np.float32(16)) / np.float32(math.log(128.0)) * np.float32(16)
    large = 16 + (np.rint(val).astype(np.int32) if (BUCKET_RINT and not force_trunc) else val.astype(np.int32))
    large = np.minimum(large, 31)
    return np.where(d < 16, d, large)


def bucket_ranges(dmax, dil, force_trunc=False):
    rel = np.arange(0, dmax + 1)
    b = t5_bucket_np(rel * dil, force_trunc)
    out = []
    st = 0
    for i in range(1, len(rel) + 1):
        if i == len(rel) or b[i] != b[st]:
            out.append((int(b[st]), st, i))
            st = i
    return out


class View:
    __slots__ = ("name", "ap", "p0", "p1", "f0", "f1")

    def __init__(self, name, ap, p0, p1, f0, f1):
        self.name, self.ap, self.p0, self.p1, self.f0, self.f1 = name, ap, p0, p1, f0, f1


class Buf:
    def __init__(self, name, t, P, F, track=True):
        self.name, self.t, self.P, self.F, self.track = name, t, P, F, track
        self.th = t[:].tensor if hasattr(t, "__getitem__") else t

    def v(self, p0, p1, f0, f1, step=1):
        if step == 1:
            ap = bass.AP(self.th, p0 * self.F + f0, [[self.F, p1 - p0], [1, f1 - f0]])
            return View(self.name, ap, p0, p1, f0, f1)
        n = (f1 - f0 + step - 1) // step
        ap = bass.AP(self.th, p0 * self.F + f0, [[self.F, p1 - p0], [step, n]])
        return View(self.name, ap, p0, p1, f0, f0 + (n - 1) * step + 1)

    def v3(self, p0, p1, f0, n1, s1, n2, s2=1):
        ap = bass.AP(self.th, p0 * self.F + f0, [[self.F, p1 - p0], [s1, n1], [s2, n2]])
        return View(self.name, ap, p0, p1, f0, f0 + (n1 - 1) * s1 + (n2 - 1) * s2 + 1)


class DBuf:
    def __init__(self, name, th, track=False):
        self.name, self.th, self.track = name, th, track

    def v(self, off, pattern):
        ap = bass.AP(self.th, off, [list(p) for p in pattern])
        hi = off + sum((n - 1) * abs(s) for s, n in pattern) + 1
        return View(self.name if self.track else None, ap, 0, 1, off, hi)


ENGS = ("pe", "act", "dve", "pool", "sp")
NDMASEM = 8


class Prog:
    def __init__(self):
        self.ops = []
        self.phase = 0
        self.pslast = {}
        self.marks = []
        self.hist = {}

    def new_phase(self):
        self.phase += 1

    def mark(self, label):
        self.marks.append((label, sum(1 for o in self.ops if o['eng'] == 'pe')))

    def op(self, eng, fn, reads=(), writes=(), dma=False):
        idx = len(self.ops)
        deps = set()
        pacc = {}
        for r in reads:
            if r.name is not None and r.name.startswith("ps"):
                pacc.setdefault(r.name, False)
        for w in writes:
            if w.name is not None and w.name.startswith("ps"):
                pacc[w.name] = True
        for bank, is_w in pacc.items():
            last = self.pslast.setdefault(bank, {})
            for e2, (i2, w2) in last.items():
                if e2 != eng or is_w or w2:
                    deps.add(i2)
            last[eng] = (idx, is_w)
        reads = [r for r in reads if not (r.name is not None and r.name.startswith("ps"))]
        writes = [w for w in writes if not (w.name is not None and w.name.startswith("ps"))]
        for r in reads:
            if r.name is None:
                continue
            for e in self.hist.get(r.name, ()):
                if e[5] and e[0] < r.p1 and r.p0 < e[1] and e[2] < r.f1 and r.f0 < e[3]:
                    deps.add(e[4])
        for w in writes:
            if w.name is None:
                continue
            for e in self.hist.get(w.name, ()):
                if e[0] < w.p1 and w.p0 < e[1] and e[2] < w.f1 and w.f0 < e[3]:
                    deps.add(e[4])
        for w in writes:
            if w.name is None:
                continue
            h = self.hist.setdefault(w.name, [])
            h[:] = [e for e in h if not (w.p0 <= e[0] and e[1] <= w.p1 and w.f0 <= e[2] and e[3] <= w.f1)]
            h.append([w.p0, w.p1, w.f0, w.f1, idx, True, eng, dma])
        for r in reads:
            if r.name is None:
                continue
            h = self.hist.setdefault(r.name, [])
            merged = False
            if not dma:
                for e in h:
                    if (not e[5]) and e[6] == eng and not e[7] and e[0] == r.p0 and e[1] == r.p1 and e[2] == r.f0 and e[3] == r.f1:
                        e[4] = idx
                        merged = True
                        break
            if not merged:
                h.append([r.p0, r.p1, r.f0, r.f1, idx, False, eng, dma])
                if len(h) > 96:
                    self._compact(h)
        self.ops.append(dict(eng=eng, fn=fn, deps=deps, dma=dma, phase=self.phase, sig=False))
        return idx

    def _compact(self, h):
        keep = [e for e in h if e[5] or e[7]]
        per = {}
        for e in h:
            if e[5] or e[7]:
                continue
            m = per.get(e[6])
            if m is None:
                per[e[6]] = list(e)
            else:
                m[0] = min(m[0], e[0]); m[1] = max(m[1], e[1]); m[2] = min(m[2], e[2]); m[3] = max(m[3], e[3])
                m[4] = max(m[4], e[4])
        h[:] = keep + list(per.values())

    def emit(self, nc, es):
        ops = self.ops
        nph = self.phase + 1
        for o in ops:
            for d in o["deps"]:
                p = ops[d]
                if p["dma"]:
                    continue
                if p["eng"] == "pe" and o["eng"] == "pe" and not o["dma"]:
                    continue
                p["sig"] = True
        sems = {}
        for e in ENGS:
            for ph in range(nph):
                sems[(e, ph)] = es.enter_context(nc.semaphore(f"s_{e}_{ph}"))
        dsems = {}
        for e in ("sp", "pool", "act"):
            for k in range(NDMASEM):
                dsems[(e, k)] = es.enter_context(nc.semaphore(f"d_{e}_{k}"))
        cnt = {}
        dcnt = {}
        dn = {e: 0 for e in ENGS}
        for o in ops:
            if o["dma"]:
                k = dn[o["eng"]] % NDMASEM
                dn[o["eng"]] += 1
                key = (o["eng"], k)
                o["dprev"] = dcnt.get(key, 0)
                dcnt[key] = o["dprev"] + 16
                o["dsem"] = key
                o["dval"] = dcnt[key]
            elif o["sig"]:
                key = (o["eng"], o["phase"])
                cnt[key] = cnt.get(key, 0) + 1
                o["sval"] = cnt[key]
        per_eng = {e: [] for e in ENGS}
        for i, o in enumerate(ops):
            per_eng[o["eng"]].append(i)
        block = es.enter_context(nc.Block())

        def run(engname, e):
            waited = {}
            for i in per_eng[engname]:
                o = ops[i]
                need = {}
                for d in o["deps"]:
                    p = ops[d]
                    if p["dma"]:
                        key = ("d",) + p["dsem"]
                        val = p["dval"]
                    else:
                        if p["eng"] == "pe" and engname == "pe" and not o["dma"]:
                            continue
                        key = ("s", p["eng"], p["phase"])
                        val = p["sval"]
                    if need.get(key, 0) < val:
                        need[key] = val
                if o["dma"] and o["dprev"] > 0:
                    key = ("d",) + o["dsem"]
                    if need.get(key, 0) < o["dprev"]:
                        need[key] = o["dprev"]
                for key, val in need.items():
                    if waited.get(key, 0) >= val:
                        continue
                    waited[key] = val
                    sem = dsems[key[1:]] if key[0] == "d" else sems[key[1:]]
                    e.wait_ge(sem, val)
                ins = o["fn"](e)
                if ins is None:
                    continue
                if o["dma"]:
                    ins.then_inc(dsems[o["dsem"]], 16)
                elif o["sig"]:
                    ins.then_inc(sems[(engname, o["phase"])], 1)

        @block.tensor
        def _(e):
            run("pe", e)

        @block.scalar
        def _(e):
            run("act", e)

        @block.vector
        def _(e):
            run("dve", e)

        @block.gpsimd
        def _(e):
            run("pool", e)

        @block.sync
        def _(e):
            run("sp", e)


def build_program(debug_stage=None):
    nc = bass.Bass("TRN2", target_bir_lowering=False)
    SKIP = set((debug_stage or '').split('+')[1:])
    P = Prog()
    es = ExitStack()

    def din(name, shape):
        return DBuf(name, nc.dram_tensor(name, list(shape), F32, kind="ExternalInput").ap().tensor)

    x_d = din("x", (NSEQ * S, D))
    relb_d = din("rel_bias", (32, 10))
    win_d = din("w_in", (DEPTH * D, INC))
    qng_d = din("q_norm_g", (DEPTH, 256))
    kvng_d = din("kv_norm_g", (DEPTH, 128))
    wuq_d = din("w_uq", (DEPTH * 256, 576))
    wukv_d = din("w_ukv", (DEPTH * 128, 768))
    dlam_d = din("diff_lambda", (DEPTH, 128))
    subg_d = din("subln_g", (DEPTH, 64))
    wo_d = din("w_o", (DEPTH * 1024, 1024))
    ln1g_d = din("ln1_g", (DEPTH, 1024))
    ln1b_d = din("ln1_b", (DEPTH, 1024))
    ln2g_d = din("ln2_g", (DEPTH, 1024))
    ln2b_d = din("ln2_b", (DEPTH, 1024))
    wg_d = din("w_gate", (DEPTH * D, FF))
    wu_d = din("w_up", (DEPTH * D, FF))
    wd_d = din("w_down", (DEPTH * FF, D))
    cst_d = din("cst", (4 * 128, 128))
    cs_d = din("cs", (2 * 32, S))
    out_d = DBuf("out", nc.dram_tensor("out", [NSEQ * S, D], F32, kind="ExternalOutput").ap().tensor, track=True)
    if 'nointernal' not in SKIP:
        gsc_d = DBuf("gsc", nc.dram_tensor("gsc", [10, 2176 + 3 * 384], F32, kind="Internal").ap().tensor, track=True)
        ebb_d = DBuf("ebbd", nc.dram_tensor("ebbd", [4 * 128, 2048], BF16, kind="Internal").ap().tensor, track=True)
        eba_d = DBuf("ebad", nc.dram_tensor("ebad", [6 * 128, 768], BF16, kind="Internal").ap().tensor, track=True)

    WSC_N = 26 * 1024 * 1024
    wsc_d = DBuf("wsc", nc.dram_tensor("wsc", [WSC_N // 2048, 2048], BF16, kind="Internal").ap().tensor, track=True)
    wcache = {}
    wsc_off = [0]

    def wload(key, dst, src, n1, n2, flat=None):
        pat = [[n1 * n2, 128], [n2, n1], [1, n2]]
        castdst = dst
        if flat is not None:
            pat = [[n1 * n2, 128], [1, n1 * n2]]
            dst = flat
        if key in wcache:
            P.op("sp", lambda e, o=dst, i=wsc_d.v(wcache[key], pat): e.dma_start(out=o.ap, in_=i.ap), [wsc_d.v(wcache[key], pat)], [dst], dma=True)
        else:
            P.op("pool", lambda e, o=castdst, i=src: e.dma_start(out=o.ap, in_=i.ap), [src], [castdst], dma=True)
            off = wsc_off[0]
            wsc_off[0] += 128 * n1 * n2
            assert wsc_off[0] <= WSC_N
            wcache[key] = off
            P.op("sp", lambda e, o=wsc_d.v(off, pat), i=dst: e.dma_start(out=o.ap, in_=i.ap), [dst], [wsc_d.v(off, pat)], dma=True)

    def sb(name, F, dt):
        t = es.enter_context(nc.sbuf_tensor(name, [128, F], dt))
        return Buf(name, t, 128, F)

    xf = sb("xf", 8 * S, F32)
    xb = sb("xb", 8 * S, BF16)
    AB = sb("arena_bf", 40448, BF16)
    AF_ = sb("arena_f32", 5120, F32)
    cst = sb("cst_s", 512, F32)
    cstb = sb("cstb", 256, BF16)
    prm = sb("prm", 96, F32)
    ps = []
    for i in range(8):
        t = es.enter_context(nc.psum_tensor(f"ps{i}", [128, 512], F32))
        ps.append(Buf(f"ps{i}", t, 128, 512))

    ident = cst.v(0, 128, 0, 128)
    ones_f = lambda p0, p1, n=128: cst.v(p0, p1, 128, 128 + n)
    antiid_b = cstb.v(0, 128, 0, 128)
    tri_b = cstb.v(0, 128, 128, 256)

    def mm(out, lhsT, rhs, start=True, stop=True):
        rd = [lhsT, rhs] + ([] if start else [out])
        P.op("pe", lambda e: e.matmul(out.ap, lhsT=lhsT.ap, rhs=rhs.ap, start=start, stop=stop), rd, [out])

    def tr(out, in_):
        if 'trmm' in SKIP:
            P.op("pe", lambda e: e.matmul(out.ap, lhsT=in_.ap, rhs=ident.ap, start=True, stop=True), [in_, ident], [out])
        else:
            P.op("pe", lambda e: e.transpose(out.ap, in_.ap, ident.ap), [in_, ident], [out])

    def act(out, in_, func, scale=1.0, bias=0.0, eng="act"):
        rd = [in_]
        b = bias
        if isinstance(bias, View):
            rd.append(bias); b = bias.ap
        s_ = scale
        if isinstance(scale, View):
            rd.append(scale); s_ = scale.ap
        P.op("act", lambda e: e.activation(out.ap, in_.ap, func, bias=b, scale=s_), rd, [out])

    def _veng(eng):
        return eng

    def tt(out, in0, in1, op, eng="dve"):
        P.op(eng, lambda e: e.tensor_tensor(out.ap, in0.ap, in1.ap, op), [in0, in1], [out])

    def ts(out, in0, s1, s2, op0, op1=None, eng="dve"):
        rd = [in0]
        a1 = s1
        if isinstance(s1, View):
            rd.append(s1); a1 = s1.ap
        a2 = s2
        if isinstance(s2, View):
            rd.append(s2); a2 = s2.ap
        if op1 is None:
            P.op(eng, lambda e: e.tensor_scalar(out.ap, in0.ap, a1, None, op0), rd, [out])
        else:
            P.op(eng, lambda e: e.tensor_scalar(out.ap, in0.ap, a1, a2, op0, op1), rd, [out])

    def stt(out, in0, sc, in1, op0, op1):
        rd = [in0, in1]
        a = sc
        if isinstance(sc, View):
            rd.append(sc); a = sc.ap
        P.op("dve", lambda e: e.scalar_tensor_tensor(out.ap, in0.ap, a, in1.ap, op0, op1), rd, [out])

    def cp(out, in_, eng="dve"):
        if eng == "act":
            P.op("act", lambda e: e.copy(out.ap, in_.ap), [in_], [out])
        else:
            P.op(eng, lambda e: e.tensor_copy(out.ap, in_.ap), [in_], [out])

    def recip(out, in_):
        P.op("dve", lambda e: e.reciprocal(out.ap, in_.ap), [in_], [out])

    def memset(out, val, eng="dve"):
        P.op(eng, lambda e: e.memset(out.ap, val), [], [out])

    def dma(q, out, in_, nonc=False):
        if nonc:
            P.op(q, lambda e: e.dma_start(out=out.ap, in_=in_.ap, allow_slow_non_contiguous=True), [in_], [out], dma=True)
        else:
            P.op(q, lambda e: e.dma_start(out=out.ap, in_=in_.ap), [in_], [out], dma=True)

    o = 0
    def carve(n):
        nonlocal o
        r = o
        o += n
        return r
    HT0 = carve(3 * S)
    QK0 = [carve(2 * S), carve(2 * S)]
    VA0 = [carve(16 * 256), carve(16 * 256)]
    PT0 = carve(4 * 512)
    WIN0 = [carve(8 * 384), carve(8 * 384)]
    WO0 = carve(3 * 1024)
    WC0 = carve(2 * 576 + 2 * 576 + 384 + 384)
    EBB0 = carve(2048)
    EBA0 = carve(2 * 768)
    assert o <= 40448, o
    H20 = 0
    GU0 = H20 + NFC * 1024
    DW0 = GU0 + 4 * 8 * 256
    assert DW0 + 3 * 1024 <= 40448

    PR_LN = 0
    PR_QG = 64
    PR_KVG = 68
    PR_SUB = 70
    PR_NLAM = 72
    PR_TMP = 74

    dma("sp", cst.v3(0, 128, 0, 4, 128, 128), cst_d.v(0, [[128, 128], [128 * 128, 4], [1, 128]]))
    cp(cstb.v(0, 128, 0, 256), cst.v(0, 128, 256, 512))
    if 'noprm' not in SKIP:
        for l in range(DEPTH):
            for j, dd in enumerate((ln1g_d, ln1b_d, ln2g_d, ln2b_d)):
                dma("sp", prm.v(0, 128, PR_LN + 32 * l + 8 * j, PR_LN + 32 * l + 8 * j + 8),
                    dd.v(l * 1024, [[1, 128], [128, 8]]), nonc=True)
            dma("sp", prm.v(0, 128, PR_QG + 2 * l, PR_QG + 2 * l + 2), qng_d.v(l * 256, [[1, 128], [128, 2]]), nonc=True)
            dma("sp", prm.v(0, 128, PR_KVG + l, PR_KVG + l + 1), kvng_d.v(l * 128, [[1, 128], [1, 1]]), nonc=True)
            for hh in range(2):
                dma("sp", prm.v(64 * hh, 64 * hh + 64, PR_SUB + l, PR_SUB + l + 1), subg_d.v(l * 64, [[1, 64], [1, 1]]), nonc=True)
            lam_init = 0.8 - 0.6 * math.exp(-0.3 * l)
            ts(prm.v(0, 128, PR_SUB + l, PR_SUB + l + 1), prm.v(0, 128, PR_SUB + l, PR_SUB + l + 1), 1.0 - lam_init, None, ALU.mult)
            dl = AF_.v(0, 128, 0, 128)
            dma("sp", dl, dlam_d.v(l * 128, [[0, 128], [1, 128]]), nonc=True)
            pr = AF_.v(0, 128, 128, 192)
            tt(AF_.v(0, 128, 128, 160), AF_.v(0, 128, 0, 32), AF_.v(0, 128, 32, 64), ALU.mult)
            tt(AF_.v(0, 128, 160, 192), AF_.v(0, 128, 64, 96), AF_.v(0, 128, 96, 128), ALU.mult)
            s12 = prm.v(0, 128, PR_TMP, PR_TMP + 2)
            P.op("dve", lambda e, a=AF_.v3(0, 128, 128, 2, 32, 32), b=s12: e.tensor_reduce(b.ap, a.ap, mybir.AxisListType.X, ALU.add),
                 [pr], [s12])
            e12 = prm.v(0, 128, PR_TMP + 2, PR_TMP + 4)
            act(e12, s12, AF.Exp)
            tt(prm.v(0, 128, PR_NLAM + l, PR_NLAM + l + 1), prm.v(0, 128, PR_TMP + 3, PR_TMP + 4), prm.v(0, 128, PR_TMP + 2, PR_TMP + 3), ALU.subtract)
            ts(prm.v(0, 128, PR_NLAM + l, PR_NLAM + l + 1), prm.v(0, 128, PR_NLAM + l, PR_NLAM + l + 1), -lam_init, None, ALU.add)

    if 'nobias' not in SKIP:
        GW = 2176 + 3 * 384
        gv = AF_.v(0, 10, 256, 256 + GW)
        tabA = AF_.v(0, 6, 4096, 4128)
        tabB = AF_.v(0, 4, 4128, 4160)
        dma("sp", tabA, relb_d.v(0, [[1, 6], [10, 32]]), nonc=True)
        dma("sp", tabB, relb_d.v(6, [[1, 4], [10, 32]]), nonc=True)
        memset(AF_.v(0, 10, 256, 256 + GW), 0.0)
        zer = AF_.v(0, 10, 4160, 4160 + 640)
        memset(zer, 0.0)
        gB = AF_.v(0, 4, 256, 256 + 2176)
        for (b, r0, r1) in bucket_ranges(2047, 1, True):
            j0, j1 = r0 + 127, min(r1 + 127, 2176)
            jj = j0
            while jj < j1:
                je = min(jj + 640, j1)
                act(AF_.v(0, 4, 256 + jj, 256 + je), AF_.v(0, 4, 4160, 4160 + je - jj), AF.Exp, bias=AF_.v(0, 4, 4128 + b, 4128 + b + 1))
                jj = je
        dma("sp", gsc_d.v(0, [[GW, 4], [1, 2176]]), gB)
        gA0 = 256 + 2176
        for li, (win_, dil) in enumerate(A_PAIRS):
            for (b, r0, r1) in bucket_ranges(128, dil):
                j0, j1 = r0 + 127, r1 + 127
                act(AF_.v(0, 6, gA0 + li * 384 + j0, gA0 + li * 384 + j1), AF_.v(0, 6, 4160, 4160 + j1 - j0), AF.Exp,
                    bias=AF_.v(0, 6, 4096 + b, 4096 + b + 1))
        dma("sp", gsc_d.v(4 * GW + 2176, [[GW, 6], [1, 3 * 384]]), AF_.v(0, 6, gA0, gA0 + 3 * 384))
        for h in range(4):
            hk = AB.v(0, 128, 0, 2048)
            for half in range(2):
                dma("pool", AB.v(0, 128, half * 1024, half * 1024 + 1024), gsc_d.v(h * GW + half * 1024, [[1, 128], [1, 1024]]))
            st_ = AB.v(0, 128, 2048, 4096)
            for g in range(4):
                mm(ps[g].v(0, 128, 0, 512), antiid_b, AB.v(0, 128, g * 512, g * 512 + 512))
                cp(AB.v(0, 128, 2048 + g * 512, 2048 + g * 512 + 512), ps[g].v(0, 128, 0, 512), eng=("act" if g % 2 else "dve"))
            dma("sp", ebb_d.v(h * 128 * 2048, [[2048, 128], [1, 2048]]), st_)
        for h in range(6):
            for li in range(3):
                dma("pool", AB.v(0, 128, 4096 + li * 256, 4096 + li * 256 + 256),
                    gsc_d.v((4 + h) * GW + 2176 + li * 384, [[1, 128], [1, 256]]))
            mm(ps[4 + h % 2].v(0, 128, 0, 512), antiid_b, AB.v(0, 128, 4096, 4096 + 512))
            mm(ps[6 + h % 2].v(0, 128, 0, 256), antiid_b, AB.v(0, 128, 4096 + 512, 4096 + 768))
            cp(AB.v(0, 128, 5120, 5120 + 512), ps[4 + h % 2].v(0, 128, 0, 512))
            cp(AB.v(0, 128, 5120 + 512, 5120 + 768), ps[6 + h % 2].v(0, 128, 0, 256), eng="act")
            dma("sp", eba_d.v(h * 128 * 768, [[768, 128], [1, 768]]), AB.v(0, 128, 5120, 5120 + 768))

    def set_vaug_ones():
        for bi in range(2):
            memset(AB.v3(0, 128, VA0[bi] + 64, 32, 128, 64), 1.0, eng="pool")

    def xbv(c, t0, t1, step=1):
        return xb.v(0, 128, c * S + t0, c * S + t1, step)

    def load_win_cols(l, dst_off, col0, ncols, dst_stride, dst_col0=0):
        dstv = AB.v3(0, 128, dst_off + dst_col0, 8, dst_stride, ncols)
        srcv = win_d.v(l * D * INC + col0, [[INC, 128], [128 * INC, 8], [1, ncols]])
        wload(("win", l, col0, ncols), dstv, srcv, 8, ncols)

    evq = [0]

    def evac(out, in_):
        evq[0] += 1
        cp(out, in_, eng=("act" if evq[0] % 2 else "dve"))

    def inproj(wbase, wstride, wcol0, M, outbuf_fn, psbanks):
        for g in range(4):
            pb = ps[psbanks[g % len(psbanks)]]
            for c in range(8):
                mm(pb.v(0, M, 0, 512), AB.v(0, 128, wbase + c * wstride + wcol0, wbase + c * wstride + wcol0 + M),
                   xbv(c, g * 512, g * 512 + 512), start=(c == 0), stop=(c == 7))
            outbuf_fn(g, pb)

    def build_vaug(vabuf, wbase, wstride, wcol0, tok_fn, unit0=0):
        for blk in range(16):
            pb = ps[2 + blk % 2]
            t0, t1, st = tok_fn(blk)
            for c in range(8):
                mm(pb.v(0, 128, 0, 128), xbv(c, t0, t1, st),
                   AB.v(0, 128, wbase + c * wstride + wcol0, wbase + c * wstride + wcol0 + 128), start=(c == 0), stop=(c == 7))
            evac(AB.v3(0, 128, vabuf + blk * 256 + unit0 * 128, 2, 128, 64), pb.v3(0, 128, 0, 2, 64, 64))

    pti = [0]

    PT_BC = [PT0 + i * 512 for i in range(4)] + [EBA0 + i * 512 for i in range(3)]
    PT_A = [PT0 + i * 256 for i in range(8)]

    def next_pt(pool=None):
        pool = pool or PT_BC
        pti[0] += 1
        return pool[pti[0] % len(pool)]

    LOOK = 3

    def pipeline(s1, s2, L=LOOK):
        n = len(s1)
        for i in range(n + L):
            if i < n:
                s1[i]()
            if i - L >= 0:
                s2[i - L]()

    def stage_dump_and_finish(seq):
        store_out(seq)

    def store_out(seq):
        for t in range(16):
            stg = AF_.v(0, 128, (t % 4) * 1024, (t % 4) * 1024 + 1024)
            for half in range(2):
                pb = ps[(2 * t + half) % 8]
                for cc in range(4):
                    c = half * 4 + cc
                    tr(pb.v(0, 128, cc * 128, cc * 128 + 128), xf.v(0, 128, c * S + t * 128, c * S + t * 128 + 128))
                cp(AF_.v(0, 128, (t % 4) * 1024 + half * 512, (t % 4) * 1024 + half * 512 + 512), pb.v(0, 128, 0, 512),
                   eng=("act" if half else "dve"))
            dma("sp", out_d.v((seq * S + t * 128) * D, [[D, 128], [1, D]]), stg)

    def load_x(seq):
        for tg in range(4):
            for tt_ in range(4):
                t = tg * 4 + tt_
                dma("sp", AF_.v(0, 128, tt_ * 1024, tt_ * 1024 + 1024), x_d.v((seq * S + t * 128) * D, [[D, 128], [1, D]]))
            for c in range(8):
                pb = ps[c % 8]
                for tt_ in range(4):
                    tr(pb.v(0, 128, tt_ * 128, tt_ * 128 + 128), AF_.v(0, 128, tt_ * 1024 + c * 128, tt_ * 1024 + c * 128 + 128))
                cp(xf.v(0, 128, c * S + tg * 512, c * S + tg * 512 + 512), pb.v(0, 128, 0, 512), eng="dve")
                cp(xb.v(0, 128, c * S + tg * 512, c * S + tg * 512 + 512), xf.v(0, 128, c * S + tg * 512, c * S + tg * 512 + 512), eng="act")

    def wo_pass(l, row0, npairs, first):
        for pi in range(npairs):
            for hf in range(2):
                wload(("wo", l, row0 + pi * 128, hf), AB.v3(0, 128, WO0 + pi * 1024 + hf * 512, 1, 512, 512),
                      wo_d.v((l * 1024 + row0 + pi * 128) * 1024 + hf * 512, [[1024, 128], [512, 1], [1, 512]]), 1, 512)
        k = 0
        for c in range(8):
            for g in range(4):
                pb = ps[k % 4]
                k += 1
                for pi in range(npairs):
                    mm(pb.v(0, 128, 0, 512), AB.v(0, 128, WO0 + pi * 1024 + c * 128, WO0 + pi * 1024 + c * 128 + 128),
                       AB.v(0, 128, HT0 + pi * S + g * 512, HT0 + pi * S + g * 512 + 512), start=(pi == 0), stop=(pi == npairs - 1))
                xv = xf.v(0, 128, c * S + g * 512, c * S + g * 512 + 512)
                if first:
                    stt(xv, xv, ALPHA, pb.v(0, 128, 0, 512), ALU.mult, ALU.add)
                else:
                    tt(xv, xv, pb.v(0, 128, 0, 512), ALU.add)

    def layer_norm(l, which):
        gcol = PR_LN + 32 * l + (0 if which == 1 else 16)
        bcol = gcol + 8
        for g in range(4):
            s1 = ps[(2 * g) % 4]
            s2 = ps[(2 * g + 1) % 4]
            for c in range(8):
                yv = xf.v(0, 128, c * S + g * 512, c * S + g * 512 + 512)
                sq = AF_.v(0, 128, (c % 2) * 512, (c % 2) * 512 + 512)
                act(sq, yv, AF.Square)
                mm(s1.v(0, 128, 0, 512), ones_f(0, 128), yv, start=(c == 0), stop=(c == 7))
                mm(s2.v(0, 128, 0, 512), ones_f(0, 128), sq, start=(c == 0), stop=(c == 7))
            mean = AF_.v(0, 128, 1024, 1536)
            ts(mean, s1.v(0, 128, 0, 512), 1.0 / D, None, ALU.mult)
            msq = AF_.v(0, 128, 1536, 2048)
            tt(msq, mean, mean, ALU.mult)
            var = AF_.v(0, 128, 2048, 2560)
            stt(var, s2.v(0, 128, 0, 512), 1.0 / D, msq, ALU.mult, ALU.subtract)
            lnv = AF_.v(0, 128, 2560, 3072)
            act(lnv, var, AF.Ln, bias=prm_eps(LN_EPS))
            rstd = ps[4 + g % 2].v(0, 128, 0, 512)
            act(rstd, lnv, AF.Exp, scale=-0.5)
            mr = ps[6 + g % 2].v(0, 128, 0, 512)
            tt(mr, mean, rstd, ALU.mult)
            for c in range(8):
                yv = xf.v(0, 128, c * S + g * 512, c * S + g * 512 + 512)
                t1 = AF_.v(0, 128, 3072 + (c % 2) * 1024, 3072 + (c % 2) * 1024 + 512)
                tt(t1, yv, rstd, ALU.mult)
                t2 = AF_.v(0, 128, 3072 + (c % 2) * 1024 + 512, 3072 + (c % 2) * 1024 + 1024)
                tt(t2, t1, mr, ALU.subtract)
                act(yv, t2, AF.Identity, scale=prm.v(0, 128, gcol + c, gcol + c + 1), bias=prm.v(0, 128, bcol + c, bcol + c + 1))
                ts(xb.v(0, 128, c * S + g * 512, c * S + g * 512 + 512), t2, prm.v(0, 128, gcol + c, gcol + c + 1),
                   prm.v(0, 128, bcol + c, bcol + c + 1), ALU.mult, ALU.add, eng="pool")

    epsv = {}

    def prm_eps(val, p0=0, p1=128):
        if val not in epsv:
            col = 80 + len(epsv)
            memset(prm.v(0, 128, col, col + 1), val)
            epsv[val] = col
        col = epsv[val]
        return prm.v(p0, p1, col, col + 1)

    def ffn(l):
        for tg in range(2):
            T0 = tg * 1024
            gi = 0
            for fb in range(11):
                gb = GU0 + (fb % 2) * 2 * 2048
                for wi, wd in enumerate((wg_d, wu_d)):
                    wload(("gu", l, wi, fb), AB.v3(0, 128, gb + wi * 2048, 8, 256, 256),
                          wd.v(l * D * FF + fb * 256, [[FF, 128], [128 * FF, 8], [1, 256]]), 8, 256,
                          flat=AB.v(0, 128, gb + wi * 2048, gb + wi * 2048 + 2048))
                for fc in range(2):
                    f = fb * 2 + fc
                    for half in range(2):
                        pg = ps[(gi % 2) * 2]
                        pu = ps[(gi % 2) * 2 + 1]
                        gi += 1
                        for c in range(8):
                            mm(pg.v(0, 128, 0, 512), AB.v(0, 128, gb + c * 256 + fc * 128, gb + c * 256 + fc * 128 + 128),
                               xbv(c, T0 + half * 512, T0 + half * 512 + 512), start=(c == 0), stop=(c == 7))
                        for c in range(8):
                            mm(pu.v(0, 128, 0, 512), AB.v(0, 128, gb + 2048 + c * 256 + fc * 128, gb + 2048 + c * 256 + fc * 128 + 128),
                               xbv(c, T0 + half * 512, T0 + half * 512 + 512), start=(c == 0), stop=(c == 7))
                        sg = AF_.v(0, 128, (gi % 2) * 512, (gi % 2) * 512 + 512)
                        act(sg, pg.v(0, 128, 0, 512), AF.Silu)
                        tt(AB.v(0, 128, H20 + f * 1024 + half * 512, H20 + f * 1024 + half * 512 + 512), sg, pu.v(0, 128, 0, 512), ALU.mult)
            di = 0
            for dp in range(4):
                for f2 in range(11):
                    db = DW0 + (di % 3) * 1024
                    di += 1
                    wload(("dw", l, dp, f2), AB.v3(0, 128, db, 2, 256, 256),
                          wd_d.v((l * FF + f2 * 256) * D + dp * 256, [[D, 128], [128 * D, 2], [1, 256]]), 2, 256,
                          flat=AB.v(0, 128, db, db + 512))
                    for fc in range(2):
                        f = f2 * 2 + fc
                        for cc in range(2):
                            for half in range(2):
                                mm(ps[4 + cc * 2 + half].v(0, 128, 0, 512), AB.v(0, 128, db + fc * 256 + cc * 128, db + fc * 256 + cc * 128 + 128),
                                   AB.v(0, 128, H20 + f * 1024 + half * 512, H20 + f * 1024 + half * 512 + 512),
                                   start=(f == 0), stop=(f == NFC - 1))
                for cc in range(2):
                    c = dp * 2 + cc
                    for half in range(2):
                        xv = xf.v(0, 128, c * S + T0 + half * 512, c * S + T0 + half * 512 + 512)
                        stt(xv, xv, ALPHA, ps[4 + cc * 2 + half].v(0, 128, 0, 512), ALU.mult, ALU.add)

    vac = [0]

    def attn_A(l, j):
        wb_ = WIN0[j % 2]
        qk = QK0[j % 2]
        load_win_cols(l, wb_, 128 * j, 128, 384, 0)
        load_win_cols(l, wb_, 384 + 128 * j, 128, 384, 128)
        load_win_cols(l, wb_, 768 + 128 * j, 128, 384, 256)
        dma("sp", AB.v3(0, 128, EBA0, 2, 768, 768), eba_d.v(2 * j * 128 * 768, [[768, 128], [128 * 768, 2], [1, 768]]))
        inproj(wb_, 384, 0, 128, lambda g, pb: evac(AB.v(0, 128, qk + g * 512, qk + g * 512 + 512), pb.v(0, 128, 0, 512)), (0, 1))
        inproj(wb_, 384, 128, 128, lambda g, pb: evac(AB.v(0, 128, qk + S + g * 512, qk + S + g * 512 + 512), pb.v(0, 128, 0, 512)), (0, 1))
        for li in (2, 0, 1):
            win_, dil = A_PAIRS[li]
            ncls = dil
            nb = 16 // dil

            def tok(blk):
                r, m = blk // nb, blk % nb
                t0 = r + dil * 128 * m
                return t0, t0 + dil * 127 + 1, dil
            vac[0] += 1
            va = VA0[vac[0] % 2]
            build_vaug(va, wb_, 384, 256, tok)
            for e in range(2):
                acc = AF_
                s1, s2 = [], []
                for blk in range(16):
                    st = {}

                    def f1(blk=blk, st=st):
                        r, m = blk // nb, blk % nb
                        nq = 256 if m < nb - 1 else 128
                        t0 = r + dil * 128 * m
                        kT = AB.v(64 * e, 64 * e + 64, qk + S + t0, qk + S + t0 + dil * 127 + 1, dil)
                        qT = AB.v(64 * e, 64 * e + 64, qk + t0, qk + t0 + dil * (nq - 1) + 1, dil)
                        pS = ps[(4, 5, 2, 3)[blk % 4]]
                        mm(pS.v(0, 128, 0, nq), kT, qT)
                        pt = next_pt(PT_A)
                        st["pt"] = pt
                        act(AB.v(0, 128, pt, pt + nq), pS.v(0, 128, 0, nq), AF.Exp, scale=0.125)
                        tt(AB.v(0, 128, pt, pt + nq), AB.v(0, 128, pt, pt + nq),
                           AB.v(0, 128, EBA0 + e * 768 + li * 256, EBA0 + e * 768 + li * 256 + nq), ALU.mult)

                    def f2(blk=blk, st=st):
                        r, m = blk // nb, blk % nb
                        nq = 256 if m < nb - 1 else 128
                        t0 = r + dil * 128 * m
                        pt = st["pt"]
                        pO = ps[6 + blk % 2]
                        mm(pO.v(0, 128, 0, nq), AB.v(0, 128, va + blk * 256 + 64 * e, va + blk * 256 + 64 * e + 128), AB.v(0, 128, pt, pt + nq))
                        av = AF_.v(0, 128, e * S + t0, e * S + t0 + dil * (nq - 1) + 1, dil)
                        if li == 2:
                            cp(av, pO.v(0, 128, 0, nq))
                        else:
                            tt(av, av, pO.v(0, 128, 0, nq), ALU.add)
                    s1.append(f1)
                    s2.append(f2)
                pipeline(s1, s2)
        for e in range(2):
            orow, lrow = (0, 64) if e == 0 else (64, 0)
            for g in range(4):
                rc = ps[g % 2].v(orow, orow + 64, 0, 512)
                recip(rc, AF_.v(lrow, lrow + 64, e * S + g * 512, e * S + g * 512 + 512))
                tt(AB.v(orow, orow + 64, HT0 + j * S + g * 512, HT0 + j * S + g * 512 + 512),
                   AF_.v(orow, orow + 64, e * S + g * 512, e * S + g * 512 + 512), rc, ALU.mult)

    def causal_group(g, scale, qv_fn, kv_fn, va, vaoff, strip_off, pO, psS=(4, 5, 0, 1)):
        nkb = 4 * g + 4
        s1, s2 = [], []
        for kb in range(nkb):
            jd = kb - 4 * g
            q0 = g * 512 + (128 * jd if jd > 0 else 0)
            nq = g * 512 + 512 - q0
            st = {}

            def f1(kb=kb, jd=jd, q0=q0, nq=nq, st=st):
                pS = ps[psS[kb % len(psS)]]
                mm(pS.v(0, 128, 0, nq), kv_fn(kb * 128, kb * 128 + 128), qv_fn(q0, q0 + nq))
                pt = next_pt()
                st["pt"] = pt
                act(AB.v(0, 128, pt, pt + nq), pS.v(0, 128, 0, nq), AF.Exp, scale=scale)
                if strip_off is not None:
                    so = strip_off + q0 - 128 * kb
                    tt(AB.v(0, 128, pt, pt + nq), AB.v(0, 128, pt, pt + nq), AB.v(0, 128, so, so + nq), ALU.mult)
                elif jd >= 0:
                    tt(AB.v(0, 128, pt, pt + 128), AB.v(0, 128, pt, pt + 128), tri_b, ALU.mult)

            def f2(kb=kb, q0=q0, nq=nq, st=st):
                pt = st["pt"]
                mm(pO.v(0, 128, q0 - g * 512, q0 - g * 512 + nq), AB.v(0, 128, va + kb * 256 + vaoff, va + kb * 256 + vaoff + 128),
                   AB.v(0, 128, pt, pt + nq), start=(kb == 0), stop=(kb == nkb - 1))
            s1.append(f1)
            s2.append(f2)
        pipeline(s1, s2)

    def attn_B(l, p):
        wb_ = WIN0[p % 2]
        va = VA0[p % 2]
        b0 = 1152
        load_win_cols(l, wb_, b0 + 512 + 128 * p, 128, 384, 256)
        build_vaug(va, wb_, 384, 256, lambda blk: (blk * 128, blk * 128 + 128, 1))
        for e in range(2):
            h = 2 * p + e
            qk = QK0[h % 2]
            load_win_cols(l, wb_, b0 + 64 * h, 64, 384, 0 + 64 * e)
            load_win_cols(l, wb_, b0 + 256 + 64 * h, 64, 384, 128 + 64 * e)
            dma("sp", AB.v(0, 128, EBB0, EBB0 + 2048), ebb_d.v(h * 128 * 2048, [[2048, 128], [1, 2048]]))
            inproj(wb_, 384, 64 * e, 64, lambda g, pb: evac(AB.v(0, 64, qk + g * 512, qk + g * 512 + 512), pb.v(0, 64, 0, 512)), (0, 1))
            inproj(wb_, 384, 128 + 64 * e, 64, lambda g, pb: evac(AB.v(0, 64, qk + S + g * 512, qk + S + g * 512 + 512), pb.v(0, 64, 0, 512)), (0, 1))
            orow, lrow = (0, 64) if e == 0 else (64, 0)
            for g in range(4):
                ons = []
                for m in range(2):
                    pO = ps[(6 if g % 2 == 0 else 2) + m]
                    causal_group(g, 32 ** -0.5,
                                 lambda a, b, m=m: AB.v(32 * m, 32 * m + 32, qk + a, qk + b),
                                 lambda a, b, m=m: AB.v(32 * m, 32 * m + 32, qk + S + a, qk + S + b),
                                 va, 64 * e, EBB0, pO)
                    sl = (m * 2 + g % 2) * 512
                    rcv = AF_.v(orow, orow + 64, sl, sl + 512)
                    recip(rcv, pO.v(lrow, lrow + 64, 0, 512))
                    on = AF_.v(orow, orow + 64, 2048 + sl, 2048 + sl + 512)
                    tt(on, pO.v(orow, orow + 64, 0, 512), rcv, ALU.mult)
                    ons.append(on)
                ob = ons[1]
                stt(ob, ons[1], prm.v(orow, orow + 64, PR_NLAM + l, PR_NLAM + l + 1), ons[0], ALU.mult, ALU.add)
                sq = AF_.v(orow, orow + 64, 4096 + (g % 2) * 512, 4096 + (g % 2) * 512 + 512)
                act(sq, ob, AF.Square)
                pm = ps[g % 2]
                mm(pm.v(0, 128, 0, 512), ones_f(orow, orow + 64), sq)
                act(sq, pm.v(orow, orow + 64, 0, 512), AF.Ln, scale=1.0 / 64, bias=prm_eps(1e-5, orow, orow + 64))
                act(sq, sq, AF.Exp, scale=-0.5)
                stt(AB.v(orow, orow + 64, HT0 + p * S + g * 512, HT0 + p * S + g * 512 + 512), ob,
                    prm.v(orow, orow + 64, PR_SUB + l, PR_SUB + l + 1), sq, ALU.mult, ALU.mult)

    def attn_C(l):
        c0 = 1920
        lrow = l * 256
        WUQ = WC0
        WUQR = WC0 + 1152
        WKK = WC0 + 2304
        WKV = WC0 + 2304 + 384
        memset(AB.v(0, 128, WUQR, WUQR + 1152), 0.0, eng="pool")
        for c in range(2):
            dma("pool", AB.v(0, 128, WUQ + c * 576, WUQ + c * 576 + 576), wuq_d.v((l * 256 + c * 128) * 576, [[576, 128], [1, 576]]))
            for hf in range(2):
                dma("pool", AB.v3(0, 128, WUQR + c * 576 + 64 + 16 * hf, 6, 96, 16),
                    wuq_d.v((l * 256 + c * 128) * 576 + 64 + 16 * (1 - hf), [[576, 128], [96, 6], [1, 16]]))
        dma("pool", AB.v3(0, 128, WKK, 6, 64, 64), wukv_d.v(l * 128 * 768, [[768, 128], [128, 6], [1, 64]]))
        dma("pool", AB.v3(0, 128, WKV, 6, 64, 64), wukv_d.v(l * 128 * 768 + 64, [[768, 128], [128, 6], [1, 64]]))
        dma("sp", AF_.v(64, 96, 0, S), cs_d.v(0, [[S, 32], [1, S]]))
        dma("sp", AF_.v(64, 96, S, 2 * S), cs_d.v(32 * S, [[S, 32], [1, S]]))
        w0, w1 = WIN0[0], WIN0[1]
        load_win_cols(l, w0, c0, 256, 384, 0)
        memset(AB.v(0, 128, w1, w1 + 8 * 384), 0.0, eng="pool")
        load_win_cols(l, w1, c0 + 256, 128, 384, 0)
        load_win_cols(l, w1, c0 + 384, 32, 384, 128 + 64)
        load_win_cols(l, w1, c0 + 384 + 16, 16, 384, 224 + 64)
        load_win_cols(l, w1, c0 + 384, 16, 384, 224 + 64 + 16)
        CQN = QK0[0]
        CKVN = QK0[1]
        KC = [WO0, WO0]
        QH = [QK0[1] + S, EBB0]
        for g in range(4):
            tsl = (g * 512, g * 512 + 512)
            for ch in range(2):
                for c in range(8):
                    mm(ps[ch].v(0, 128, 0, 512), AB.v(0, 128, w0 + c * 384 + ch * 128, w0 + c * 384 + ch * 128 + 128),
                       xbv(c, *tsl), start=(c == 0), stop=(c == 7))
                sq = AF_.v(0, 128, 4096 + ch * 512, 4096 + ch * 512 + 512)
                act(sq, ps[ch].v(0, 128, 0, 512), AF.Square)
                mm(ps[2].v(0, 128, 0, 512), ones_f(0, 128), sq, start=(ch == 0), stop=(ch == 1))
            rs = AF_.v(0, 128, 4096, 4608)
            act(rs, ps[2].v(0, 128, 0, 512), AF.Ln, scale=1.0 / 256, bias=prm_eps(1e-6))
            act(rs, rs, AF.Exp, scale=-0.5)
            for ch in range(2):
                stt(AB.v(0, 128, CQN + ch * S + tsl[0], CQN + ch * S + tsl[1]), ps[ch].v(0, 128, 0, 512),
                    prm.v(0, 128, PR_QG + 2 * l + ch, PR_QG + 2 * l + ch + 1), rs, ALU.mult, ALU.mult)
            for c in range(8):
                mm(ps[3].v(0, 128, 0, 512), AB.v(0, 128, w1 + c * 384, w1 + c * 384 + 128), xbv(c, *tsl), start=(c == 0), stop=(c == 7))
            sq = AF_.v(0, 128, 4608, 5120)
            act(sq, ps[3].v(0, 128, 0, 512), AF.Square)
            mm(ps[2].v(0, 128, 0, 512), ones_f(0, 128), sq)
            act(sq, ps[2].v(0, 128, 0, 512), AF.Ln, scale=1.0 / 128, bias=prm_eps(1e-6))
            act(sq, sq, AF.Exp, scale=-0.5)
            stt(AB.v(0, 128, CKVN + tsl[0], CKVN + tsl[1]), ps[3].v(0, 128, 0, 512),
                prm.v(0, 128, PR_KVG + l, PR_KVG + l + 1), sq, ALU.mult, ALU.mult)
            for c in range(8):
                mm(ps[6].v(0, 96, 0, 512), AB.v(0, 128, w1 + c * 384 + 128, w1 + c * 384 + 224), xbv(c, *tsl), start=(c == 0), stop=(c == 7))
            for c in range(8):
                mm(ps[7].v(0, 96, 0, 512), AB.v(0, 128, w1 + c * 384 + 224, w1 + c * 384 + 320), xbv(c, *tsl), start=(c == 0), stop=(c == 7))
            t1 = AF_.v(64, 96, 4096, 4608)
            tt(t1, ps[6].v(64, 96, 0, 512), AF_.v(64, 96, tsl[0], tsl[1]), ALU.mult)
            t2 = AF_.v(64, 96, 4608, 5120)
            tt(t2, ps[7].v(64, 96, 0, 512), AF_.v(64, 96, S + tsl[0], S + tsl[1]), ALU.mult)
            tt(AB.v(64, 96, KC[0] + tsl[0], KC[0] + tsl[1]), t1, t2, ALU.add)
        for h in range(6):
            pp, e = h // 2, h % 2
            va = VA0[pp % 2]
            if e == 0:
                for blk in range(16):
                    pb = ps[2 + blk % 2]
                    mm(pb.v(0, 128, 0, 128), AB.v(0, 128, CKVN + blk * 128, CKVN + blk * 128 + 128),
                       AB.v(0, 128, WKV + pp * 128, WKV + pp * 128 + 128))
                    evac(AB.v3(0, 128, va + blk * 256, 2, 128, 64), pb.v3(0, 128, 0, 2, 64, 64))
            kc = KC[h % 2]
            qh = QH[h % 2]
            for g in range(4):
                tsl = (g * 512, g * 512 + 512)
                for ch in range(2):
                    mm(ps[0].v(0, 96, 0, 512), AB.v(0, 128, WUQ + ch * 576 + h * 96, WUQ + ch * 576 + h * 96 + 96),
                       AB.v(0, 128, CQN + ch * S + tsl[0], CQN + ch * S + tsl[1]), start=(ch == 0), stop=(ch == 1))
                for ch in range(2):
                    mm(ps[1].v(0, 96, 0, 512), AB.v(0, 128, WUQR + ch * 576 + h * 96, WUQR + ch * 576 + h * 96 + 96),
                       AB.v(0, 128, CQN + ch * S + tsl[0], CQN + ch * S + tsl[1]), start=(ch == 0), stop=(ch == 1))
                evac(AB.v(0, 64, qh + tsl[0], qh + tsl[1]), ps[0].v(0, 64, 0, 512))
                t1 = AF_.v(64, 96, 4096, 4608)
                tt(t1, ps[0].v(64, 96, 0, 512), AF_.v(64, 96, tsl[0], tsl[1]), ALU.mult)
                t2 = AF_.v(64, 96, 4608, 5120)
                tt(t2, ps[1].v(64, 96, 0, 512), AF_.v(64, 96, S + tsl[0], S + tsl[1]), ALU.mult)
                tt(AB.v(64, 96, qh + tsl[0], qh + tsl[1]), t1, t2, ALU.add)
                mm(ps[2 + g % 2].v(0, 64, 0, 512), AB.v(0, 128, WKK + h * 64, WKK + h * 64 + 64), AB.v(0, 128, CKVN + tsl[0], CKVN + tsl[1]))
                evac(AB.v(0, 64, kc + tsl[0], kc + tsl[1]), ps[2 + g % 2].v(0, 64, 0, 512))
            orow, lrow_ = (0, 64) if e == 0 else (64, 0)

            for g in range(4):
                pO = ps[6 + g % 2]
                causal_group(g, 96 ** -0.5,
                             lambda a, b: AB.v(0, 96, qh + a, qh + b),
                             lambda a, b: AB.v(0, 96, kc + a, kc + b),
                             va, 64 * e, None, pO)
                rcv = AF_.v(orow, orow + 64, 4096 + (g % 2) * 512, 4096 + (g % 2) * 512 + 512)
                recip(rcv, pO.v(lrow_, lrow_ + 64, 0, 512))
                tt(AB.v(orow, orow + 64, HT0 + pp * S + g * 512, HT0 + pp * S + g * 512 + 512), pO.v(orow, orow + 64, 0, 512), rcv, ALU.mult)

    stages = (debug_stage or "full").split('+')[0]
    for seq in range(NSEQ):
        if stages != "full" and seq > 0:
            break
        if 'noload' not in SKIP:
            load_x(seq)
        for l in range(DEPTH):
            P.new_phase()
            if stages == "load":
                break
            set_vaug_ones()
            P.mark(f"s{seq}l{l}A")
            for j in range(3):
                attn_A(l, j)
            P.mark(f"s{seq}l{l}woA")
            wo_pass(l, 0, 3, True)
            if stages == "A":
                break
            P.mark(f"s{seq}l{l}B")
            for p in range(2):
                attn_B(l, p)
            P.mark(f"s{seq}l{l}woB")
            wo_pass(l, 384, 2, False)
            if stages == "B":
                break
            P.mark(f"s{seq}l{l}C")
            attn_C(l)
            P.mark(f"s{seq}l{l}woC")
            wo_pass(l, 640, 3, False)
            if stages == "C":
                break
            P.mark(f"s{seq}l{l}LN1")
            layer_norm(l, 1)
            if stages == "LN1":
                break
            P.mark(f"s{seq}l{l}FFN")
            ffn(l)
            P.mark(f"s{seq}l{l}LN2")
            if stages == "FFN":
                break
            layer_norm(l, 2)
            if stages == "L0":
                break
        if 'nostore' not in SKIP:
            store_out(seq)
    P.op("sp", lambda e: None, [out_d.v(0, [[1, NSEQ * S * D]])], [])
    P.emit(nc, es)
    es.close()
    nc._marks = P.marks
    return nc


_CACHE = {}


def host_consts():
    ident = np.eye(128, dtype=np.float32)
    ones = np.ones((128, 128), np.float32)
    anti = np.ascontiguousarray(ident[::-1])
    k = np.arange(128)[:, None]
    q = np.arange(128)[None, :]
    tri = (k <= q).astype(np.float32)
    cst = np.concatenate([ident, ones, anti, tri], axis=0)
    inv = (np.float32(10000.0) ** (-np.arange(16, dtype=np.float32) / np.float32(16))).astype(np.float32)
    ang = (np.arange(S, dtype=np.float32)[:, None] * inv[None, :]).astype(np.float32)
    cos = np.cos(ang).astype(np.float32).T
    sin = np.sin(ang).astype(np.float32).T
    cs = np.concatenate([cos, cos, -sin, sin], axis=0).astype(np.float32)
    return cst, cs


def kernel(x, rel_bias, w_in, q_norm_g, kv_norm_g, w_uq, w_ukv, diff_lambda, subln_g, w_o,
           ln1_g, ln1_b, ln2_g, ln2_b, w_gate, w_up, w_down):
    f = lambda a: np.ascontiguousarray(np.asarray(a, dtype=np.float32))
    key = DEBUG_STAGE or "full"
    if key not in _CACHE:
        _CACHE[key] = build_program(DEBUG_STAGE)
    nc = _CACHE[key]
    cst, cs = host_consts()
    shared = {
        "rel_bias": f(rel_bias), "w_in": f(w_in).reshape(DEPTH * D, INC), "q_norm_g": f(q_norm_g), "kv_norm_g": f(kv_norm_g),
        "w_uq": f(w_uq).reshape(DEPTH * 256, 576), "w_ukv": f(w_ukv).reshape(DEPTH * 128, 768),
        "diff_lambda": f(diff_lambda).reshape(DEPTH, 128), "subln_g": f(subln_g),
        "w_o": f(w_o).reshape(DEPTH * 1024, 1024), "ln1_g": f(ln1_g), "ln1_b": f(ln1_b), "ln2_g": f(ln2_g), "ln2_b": f(ln2_b),
        "w_gate": f(w_gate).reshape(DEPTH * D, FF), "w_up": f(w_up).reshape(DEPTH * D, FF), "w_down": f(w_down).reshape(DEPTH * FF, D),
        "cst": cst, "cs": cs,
    }
    xs = f(x)
    ncores = 1 if DEBUG_STAGE else 8
    in_maps = []
    for c in range(ncores):
        m = dict(shared)
        m["x"] = xs[NSEQ * c:NSEQ * (c + 1)].reshape(NSEQ * S, D)
        in_maps.append(m)
    res = run_bass_kernel_spmd(nc, in_maps, core_ids=list(range(ncores)))
    out = np.concatenate([np.asarray(r["out"]).reshape(NSEQ, S, D) for r in res.results], axis=0)
    return out.astype(np.float32)
```
